# Optimizing a Trainium2 kernel written in Bass

```python
import jax, jax.numpy as jnp
from jax import lax
import numpy as np

D_MODEL = 2048
BATCH = 1
SEQ = 8192
DEPTH = 1
DEC_BATCH = 8
DEC_SEQ = 32
PAST_LEN = 2048

CHUNK = 64
MIX_WIDTH = D_MODEL
CONV_CH = MIX_WIDTH // 2
CONV_WIDTH = 31
FOX_HEADS = 16
FOX_WIDTH = MIX_WIDTH - CONV_CH
FOX_HEAD_DIM = FOX_WIDTH // FOX_HEADS
N_MEM = 256
MEM_HEADS = 4
MEM_HEAD_DIM = D_MODEL // MEM_HEADS
D_FF = 5632
FFN_CONV_WIDTH = 3
Q_BLOCK = 128
IN_COLS = 2 * CONV_CH + 3 * FOX_WIDTH + FOX_HEADS
DEEPNORM_ALPHA = (2 * DEPTH) ** 0.25
DEEPNORM_BETA = (8 * DEPTH) ** -0.25
LN_EPS = 1e-5

kernel_name = "hybrid_conv_fox_stream_step"


def _layernorm(x, g, b):
    xf = x.astype(jnp.float32)
    mu = xf.mean(-1, keepdims=True)
    var = jnp.square(xf - mu).mean(-1, keepdims=True)
    return ((xf - mu) * lax.rsqrt(var + LN_EPS) * g.astype(jnp.float32) + b.astype(jnp.float32)).astype(x.dtype)


def _causal_dwconv(xp, w, b):
    y = lax.conv_general_dilated(xp, w[:, None, :], (1,), 'VALID',
                                 dimension_numbers=('NWC', 'WIO', 'NWC'),
                                 feature_group_count=xp.shape[-1])
    return y + b


def _fox_block(q, fq, qpos, k, v, fk, kpos):
    s = jnp.einsum('bqhd,bkhd->bhqk', q, k, preferred_element_type=jnp.float32) * (FOX_HEAD_DIM ** -0.5)
    s = s + jnp.transpose(fq, (0, 2, 1))[:, :, :, None] - jnp.transpose(fk, (0, 2, 1))[:, :, None, :]
    s = jnp.where(kpos[None, :] <= qpos[:, None], s, -jnp.inf)
    p = jax.nn.softmax(s, axis=-1)
    return jnp.einsum('bhqk,bkhd->bqhd', p.astype(v.dtype), v)


def _fox_attention(q, fq, k, v, fk, past):
    B, Tq, H, Dh = q.shape
    kpos = jnp.arange(k.shape[1])
    qpos = past + jnp.arange(Tq)
    if Tq % Q_BLOCK != 0:
        return _fox_block(q, fq, qpos, k, v, fk, kpos)
    nb = Tq // Q_BLOCK
    qb = q.reshape(B, nb, Q_BLOCK, H, Dh).swapaxes(0, 1)
    fb = fq.reshape(B, nb, Q_BLOCK, H).swapaxes(0, 1)
    pb = qpos.reshape(nb, Q_BLOCK)
    out = lax.map(lambda a: _fox_block(a[0], a[1], a[2], k, v, fk, kpos), (qb, fb, pb))
    return out.swapaxes(0, 1).reshape(B, Tq, H, Dh)


def _layer(x, mem_k, mem_v, conv_hist, k_hist, v_hist, lf_hist, ffn_hist,
           w_in, b_forget, w_conv, b_conv, g_conv_ln, b_conv_ln, w_out, g_ln1, b_ln1,
           w_mq, w_mo, g_ln2, b_ln2, w_up, w_ffn_conv, b_ffn_conv, w_down, g_ln3, b_ln3):
    B, T, _ = x.shape
    past = k_hist.shape[1]
    u = x @ w_in
    c1 = CONV_CH
    c2 = 2 * CONV_CH
    c3 = c2 + FOX_WIDTH
    c4 = c3 + FOX_WIDTH
    c5 = c4 + FOX_WIDTH
    a, gate, q, k, v, zf = jnp.split(u, [c1, c2, c3, c4, c5], axis=-1)
    glu = a * jax.nn.sigmoid(gate)
    conv_in = jnp.concatenate([conv_hist, glu], axis=1)
    ca = _causal_dwconv(conv_in, w_conv, b_conv)
    ca = jax.nn.silu(_layernorm(ca, g_conv_ln, b_conv_ln))
    new_conv = conv_in[:, -(CONV_WIDTH - 1):]
    q = q.reshape(B, T, FOX_HEADS, FOX_HEAD_DIM)
    k = k.reshape(B, T, FOX_HEADS, FOX_HEAD_DIM)
    v = v.reshape(B, T, FOX_HEADS, FOX_HEAD_DIM)
    logf = jax.nn.log_sigmoid(zf.astype(jnp.float32) + b_forget.astype(jnp.float32))
    k_all = jnp.concatenate([k_hist, k], axis=1)
    v_all = jnp.concatenate([v_hist, v], axis=1)
    F = jnp.cumsum(jnp.concatenate([lf_hist.astype(jnp.float32), logf], axis=1), axis=1)
    ob = _fox_attention(q, F[:, past:], k_all, v_all, F, past).reshape(B, T, FOX_WIDTH)
    mix = jnp.concatenate([ca.astype(x.dtype), ob.astype(x.dtype)], axis=-1) @ w_out
    x = _layernorm(DEEPNORM_ALPHA * x + mix, g_ln1, b_ln1)
    qm = (x @ w_mq).reshape(B, T, MEM_HEADS, MEM_HEAD_DIM)
    sm = jnp.einsum('bqhd,bmhd->bhqm', qm, mem_k, preferred_element_type=jnp.float32) * (MEM_HEAD_DIM ** -0.5)
    pm = jax.nn.softmax(sm, axis=-1).astype(mem_v.dtype)
    om = jnp.einsum('bhqm,bmhd->bqhd', pm, mem_v).reshape(B, T, D_MODEL)
    x = _layernorm(DEEPNORM_ALPHA * x + om @ w_mo, g_ln2, b_ln2)
    h = x @ w_up
    ffn_in = jnp.concatenate([ffn_hist, h], axis=1)
    hc = _causal_dwconv(ffn_in, w_ffn_conv, b_ffn_conv)
    hg, hv = jnp.split(hc, [D_FF], axis=-1)
    y = (jax.nn.silu(hg) * hv) @ w_down
    x = _layernorm(DEEPNORM_ALPHA * x + y, g_ln3, b_ln3)
    new_ffn = ffn_in[:, -(FFN_CONV_WIDTH - 1):]
    return x, new_conv, k, v, logf.astype(x.dtype), new_ffn


def setup_inputs(seed: int = 0) -> dict:
    key = jax.random.key(seed)
    ks = jax.random.split(key, 40)
    n = lambda i, shape, s: jax.random.normal(ks[i], shape, jnp.float32) * s
    L = DEPTH
    return {
        "x_prompt": n(0, (BATCH, SEQ, D_MODEL), 1.0),
        "x_sample": n(1, (DEC_BATCH, DEC_SEQ, D_MODEL), 1.0),
        "cache_conv": n(2, (L, DEC_BATCH, CONV_WIDTH - 1, CONV_CH), 0.5),
        "cache_fox_k": n(3, (L, DEC_BATCH, PAST_LEN, FOX_HEADS, FOX_HEAD_DIM), 1.0),
        "cache_fox_v": n(4, (L, DEC_BATCH, PAST_LEN, FOX_HEADS, FOX_HEAD_DIM), 1.0),
        "cache_fox_logf": jax.nn.log_sigmoid(n(5, (L, DEC_BATCH, PAST_LEN, FOX_HEADS), 1.0) + 3.0),
        "cache_mem_k": n(6, (L, DEC_BATCH, N_MEM, MEM_HEADS, MEM_HEAD_DIM), 1.0),
        "cache_mem_v": n(7, (L, DEC_BATCH, N_MEM, MEM_HEADS, MEM_HEAD_DIM), 1.0),
        "cache_ffn": n(8, (L, DEC_BATCH, FFN_CONV_WIDTH - 1, 2 * D_FF), 1.0),
        "mem_prompt": n(9, (BATCH, N_MEM, D_MODEL), 1.0),
        "w_in": n(10, (L, D_MODEL, IN_COLS), D_MODEL ** -0.5),
        "b_forget": jnp.linspace(1.0, 5.0, FOX_HEADS)[None, :] + n(11, (L, FOX_HEADS), 0.1),
        "w_conv": n(12, (L, CONV_WIDTH, CONV_CH), CONV_WIDTH ** -0.5),
        "b_conv": n(13, (L, CONV_CH), 0.02),
        "g_conv_ln": 1.0 + n(14, (L, CONV_CH), 0.02),
        "b_conv_ln": n(15, (L, CONV_CH), 0.02),
        "w_out": n(16, (L, MIX_WIDTH, D_MODEL), DEEPNORM_BETA * MIX_WIDTH ** -0.5),
        "g_ln1": 1.0 + n(17, (L, D_MODEL), 0.02),
        "b_ln1": n(18, (L, D_MODEL), 0.02),
        "w_mq": n(19, (L, D_MODEL, D_MODEL), D_MODEL ** -0.5),
        "w_mk": n(20, (L, D_MODEL, D_MODEL), D_MODEL ** -0.5),
        "w_mv": n(21, (L, D_MODEL, D_MODEL), D_MODEL ** -0.5),
        "w_mo": n(22, (L, D_MODEL, D_MODEL), DEEPNORM_BETA * D_MODEL ** -0.5),
        "g_ln2": 1.0 + n(23, (L, D_MODEL), 0.02),
        "b_ln2": n(24, (L, D_MODEL), 0.02),
        "w_up": n(25, (L, D_MODEL, 2 * D_FF), D_MODEL ** -0.5),
        "w_ffn_conv": n(26, (L, FFN_CONV_WIDTH, 2 * D_FF), FFN_CONV_WIDTH ** -0.5),
        "b_ffn_conv": n(27, (L, 2 * D_FF), 0.02),
        "w_down": n(28, (L, D_FF, D_MODEL), DEEPNORM_BETA * D_FF ** -0.5),
        "g_ln3": 1.0 + n(29, (L, D_MODEL), 0.02),
        "b_ln3": n(30, (L, D_MODEL), 0.02),
    }


def reference(x_prompt, x_sample, cache_conv, cache_fox_k, cache_fox_v, cache_fox_logf,
              cache_mem_k, cache_mem_v, cache_ffn, mem_prompt,
              w_in, b_forget, w_conv, b_conv, g_conv_ln, b_conv_ln, w_out, g_ln1, b_ln1,
              w_mq, w_mk, w_mv, w_mo, g_ln2, b_ln2, w_up, w_ffn_conv, b_ffn_conv, w_down,
              g_ln3, b_ln3):
    B = x_prompt.shape[0]
    yp, ys = x_prompt, x_sample
    pc, pk, pv, plf, pmk, pmv, pf = [], [], [], [], [], [], []
    sc, sk, sv, slf, sf = [], [], [], [], []
    for l in range(DEPTH):
        params = (w_in[l], b_forget[l], w_conv[l], b_conv[l], g_conv_ln[l], b_conv_ln[l], w_out[l],
                  g_ln1[l], b_ln1[l], w_mq[l], w_mo[l], g_ln2[l], b_ln2[l], w_up[l], w_ffn_conv[l],
                  b_ffn_conv[l], w_down[l], g_ln3[l], b_ln3[l])
        mk = (mem_prompt @ w_mk[l]).reshape(B, N_MEM, MEM_HEADS, MEM_HEAD_DIM)
        mv = (mem_prompt @ w_mv[l]).reshape(B, N_MEM, MEM_HEADS, MEM_HEAD_DIM)
        zc = jnp.zeros((B, CONV_WIDTH - 1, CONV_CH), yp.dtype)
        zk = jnp.zeros((B, 0, FOX_HEADS, FOX_HEAD_DIM), yp.dtype)
        zl = jnp.zeros((B, 0, FOX_HEADS), yp.dtype)
        zf = jnp.zeros((B, FFN_CONV_WIDTH - 1, 2 * D_FF), yp.dtype)
        yp, c_, k_, v_, lf_, f_ = _layer(yp, mk, mv, zc, zk, zk, zl, zf, *params)
        pc.append(c_); pk.append(k_); pv.append(v_); plf.append(lf_); pmk.append(mk); pmv.append(mv); pf.append(f_)
        ys, c_, k_, v_, lf_, f_ = _layer(ys, cache_mem_k[l], cache_mem_v[l], cache_conv[l], cache_fox_k[l],
                                         cache_fox_v[l], cache_fox_logf[l], cache_ffn[l], *params)
        sc.append(c_); sk.append(k_); sv.append(v_); slf.append(lf_); sf.append(f_)
    p_conv = jnp.stack(pc)
    p_fox_k = jnp.stack(pk)
    p_fox_v = jnp.stack(pv)
    p_fox_logf = jnp.stack(plf)
    p_mem_k = jnp.stack(pmk)
    p_mem_v = jnp.stack(pmv)
    p_ffn = jnp.stack(pf)
    s_conv = jnp.stack(sc)
    s_fox_k = jnp.stack(sk)
    s_fox_v = jnp.stack(sv)
    s_fox_logf = jnp.stack(slf)
    s_ffn = jnp.stack(sf)
    return (yp, ys, p_conv, p_fox_k, p_fox_v, p_fox_logf, p_mem_k, p_mem_v, p_ffn,
            s_conv, s_fox_k, s_fox_v, s_fox_logf, s_ffn)
```

```python
import numpy as np
from contextlib import ExitStack
import concourse.bass as bass
import concourse.mybir as mybir
from concourse.bass_utils import run_bass_kernel_spmd

F32 = mybir.dt.float32
BF16 = mybir.dt.bfloat16
AF = mybir.ActivationFunctionType
ALU = mybir.AluOpType
EPOCH = 6000

D = 2048
NCORE = 8
OWN = 1024
NT = 1088
NQ = 1058
HALO0 = 1024
SMP0 = 1026
CH0 = 1058
NREM = 7168
NKEY = NREM + OWN + 2 + 32 + 2048
KC_OWN = NREM
KC_HALO = NREM + OWN
KC_SMP = KC_HALO + 2
KC_CACHE = KC_SMP + 32
NBLK = 56 + 8 + 1 + 1 + 16
B_OWN, B_HALO, B_SMP, B_CACHE = 56, 64, 65, 66
DFF = 5632
NEG = -30000.0
ALPHA = 2.0 ** 0.25
LN_EPS = 1e-5


class Eng:
    def __init__(self, name, h):
        self.name, self.h = name, h
        self.sems = []
        self.count = 0
        self.seen = {}


class Buf:
    def __init__(self, name, excl=False):
        self.name = name
        self.excl = excl
        self.w = {}
        self.r = {}
        self.dsem = None
        self.dcount = 0


class _Rec:
    def __init__(self):
        self.calls = []

    def __getattr__(self, name):
        def m(*a, **k):
            self.calls.append((name, a, k))
            return None
        return m


class Prog:
    def __init__(self, nc, stack):
        self.nc, self.stack = nc, stack
        self.eng = {}
        for name, h in [("pe", nc.tensor), ("act", nc.scalar), ("dve", nc.vector),
                        ("pool", nc.gpsimd), ("sp", nc.sync)]:
            self.eng[name] = Eng(name, h)
        self.allbufs = []
        self.sempool = {}
        self.deferred = None
        self.defq = []

    def newsem(self, name):
        return self.stack.enter_context(self.nc.semaphore(name))

    def buf(self, name, excl=False):
        b = Buf(name, excl)
        self.allbufs.append(b)
        return b

    def sbuf(self, name, shape, dt, stack=None):
        t = (stack or self.stack).enter_context(self.nc.sbuf_tensor("sb_" + name, shape, dt))
        return t, self.buf(name)

    def _wait(self, E, need):
        for key, (kind, ref, val) in need.items():
            if E.seen.get(key, 0) >= val:
                continue
            if kind == "e":
                src = self.eng[ref]
                ep, v = divmod(val - 1, EPOCH)
                E.h.wait_ge(src.sems[ep], v + 1)
            else:
                E.h.wait_ge(ref, val)
            E.seen[key] = val

    @staticmethod
    def _merge(need, d):
        for k, t in d.items():
            if k not in need or need[k][2] < t[2]:
                need[k] = t

    def _deps(self, reads, writes):
        need = {}
        for b in reads:
            self._merge(need, b.w)
            if b.excl:
                self._merge(need, b.r)
        for b in writes:
            self._merge(need, b.w)
            self._merge(need, b.r)
        return need

    def op(self, e, fn, reads=(), writes=()):
        if DEAD[0]:
            return None
        if self.deferred is not None:
            rec = _Rec()
            fn(rec)
            calls = rec.calls

            def replay(h, calls=calls):
                inst = None
                for (name, a, k) in calls:
                    inst = getattr(h, name)(*a, **k)
                return inst
            self.deferred.append((e, replay, list(reads), list(writes)))
            return None
        E = self.eng[e]
        self._wait(E, self._deps(reads, writes))
        inst = fn(E.h)
        ep, v = divmod(E.count, EPOCH)
        if ep >= len(E.sems):
            E.sems.append(self.newsem(f"s_{e}{ep}"))
        inst.then_inc(E.sems[ep], 1)
        E.count += 1
        tok = ("e", e, E.count)
        for b in reads:
            b.r[e] = tok
        for b in writes:
            b.w[e] = tok
        return inst

    def dma(self, q, out, in_, reads=(), writes=(), sembuf=None, **kw):
        if DEAD[0]:
            return None
        E = self.eng[q]
        self._wait(E, self._deps(reads, writes))
        sb = sembuf or (writes[0] if writes else reads[0])
        kind = "sw" if q == "pool" else "hw"
        if sb.dsem is None:
            sb.dsem = {}
        if kind not in sb.dsem:
            pool = self.sempool.setdefault(kind, [])
            if pool:
                sb.dsem[kind] = pool.pop()
            else:
                self.nsem = getattr(self, "nsem", 0) + 1
                nm = f"d{kind}{self.nsem}"
                sb.dsem[kind] = [self.newsem(nm), 0, nm]
        ent = sb.dsem[kind]
        inst = E.h.dma_start(out=out, in_=in_, **kw)
        inst.then_inc(ent[0], 16)
        ent[1] += 16
        key = ent[2]
        tok = ("d", ent[0], ent[1])
        for b in reads:
            b.r[key] = tok
        for b in writes:
            b.w[key] = tok
        return inst

    def flush(self, n=None):
        k = len(self.defq) if n is None else min(n, len(self.defq))
        for _ in range(k):
            e, fn, reads, writes = self.defq.pop(0)
            self.op(e, fn, reads, writes)

    def barrier(self):
        if DEAD[0]:
            return
        self.flush()
        need = {}
        for b in self.allbufs:
            self._merge(need, b.w)
            self._merge(need, b.r)
        for E in self.eng.values():
            self._wait(E, need)
        for b in self.allbufs:
            if b.dsem:
                for kind, ent in b.dsem.items():
                    self.sempool.setdefault(kind, []).append(ent)
                b.dsem = None

    def finish(self, bufs, e="sp"):
        DEAD[0] = False
        E = self.eng[e]
        need = {}
        for b in bufs:
            self._merge(need, b.w)
            self._merge(need, b.r)
        self._wait(E, need)


def mm(P, out, pairs, reads, writes):
    def f(e):
        n = len(pairs)
        for i, (l, r) in enumerate(pairs):
            inst = e.matmul(out, lhsT=l, rhs=r, start=(i == 0), stop=(i == n - 1))
        return inst
    return P.op("pe", f, reads, writes)


class StopBuild(Exception):
    pass


import os
KLIMIT = int(os.environ.get("KLIMIT", "999"))


DEAD = [False]


def CK(n):
    if n > KLIMIT:
        DEAD[0] = True


def kp(ap):
    return ap.rearrange("(kc p) n -> p kc n", p=128)


def build_nc(stage=99):
    DEAD[0] = False
    nc = bass.Bass("TRN2", target_bir_lowering=False)

    def din(name, shape, dt=F32):
        return nc.dram_tensor(name, list(shape), dt, kind="ExternalInput").ap()

    def dout(name, shape, dt=F32):
        return nc.dram_tensor(name, list(shape), dt, kind="ExternalOutput").ap()

    def dscr(name, shape, dt):
        return nc.dram_tensor(name, list(shape), dt).ap()

    xT_d = din("xT", [D, NT])
    xtm_d = din("xtm", [NT, D])
    xrT_d = din("xrT", [D, NREM])
    w_in_d = din("w_in", [D, 5136])
    w_out_d = din("w_out", [D, D])
    w_mq_d = din("w_mq", [D, D])
    w_mk_d = din("w_mk", [D, D])
    w_mv_d = din("w_mv", [D, D])
    w_mo_d = din("w_mo", [D, D])
    w_up_d = din("w_up", [D, 2 * DFF])
    w_down_d = din("w_down", [DFF, D])
    wconvT_d = din("wconvT", [128, 8, 31])
    cvec_d = din("cvec", [128, 8, 3])
    wffnT_d = din("wffnT", [128, 88, 3])
    bffn_d = din("bffn", [128, 88])
    lnp_d = din("lnp", [6, 128, D])
    bfb_d = din("bfb", [128, 16])
    valid_d = din("valid", [128, 8])
    hmask_d = din("hmask", [128, 1])
    ident_d = din("ident", [128, 128])
    tri_d = din("tri", [128, 128])
    dneg_d = din("dneg", [128, 1])
    shift_d = din("shift", [64, 128])
    ckT_d = din("ckT", [1024, 2048])
    cv_d = din("cv", [2048, 1024])
    clf_d = din("clf", [2048, 16])
    ccT_d = din("ccT", [1024, 30])
    cffT_d = din("cffT", [2 * DFF, 2])
    memT_d = din("memT", [D, 256])
    cmkT_d = din("cmkT", [D, 256])
    cmv_d = din("cmv", [256, D])

    y_d = dout("y", [NQ, D])
    kT_o = dout("kTo", [1024, NQ])
    v_o = dout("vo", [NQ, 1024])
    lf_o = dout("lfo", [NQ, 16])
    convT_o = dout("convTo", [1024, 60])
    memk_o = dout("memk", [256, D])
    memv_o = dout("memv", [256, D])
    ffnT_o = dout("ffnTo", [2 * DFF, 4])

    ktr_d = dscr("ktr", [1024, KC_CACHE], BF16)
    vr_d = dscr("vr", [16, 128, B_CACHE, 65], BF16)
    qtr_d = dscr("qtr", [16, 128, NQ], BF16)
    xres_d = dscr("xres", [9 * 128, D], F32)

    outs = []
    with ExitStack() as stack:
        P = Prog(nc, stack)
        OUTB = {n: P.buf(n) for n in ["y", "kTo", "vo", "lfo", "convTo", "memk", "memv", "ffnTo"]}
        SCR = {n: P.buf(n) for n in ["ktr", "vr", "qtr", "xres"]}
        PS = []
        for i in range(8):
            t = stack.enter_context(nc.psum_tensor(f"ps{i}", [128, 512], F32))
            PS.append((t, P.buf(f"ps{i}", excl=True)))
        PST = PS[7][0][:].bitcast(BF16)
        BPST = PS[7][1]
        WS = [P.sbuf(f"ws{i}", [128, 16, 512], BF16) for i in range(3)]

        identb, Bident = P.sbuf("identb", [128, 128], BF16)
        trib, Btrib = P.sbuf("trib", [128, 128], BF16)
        tri32, Btri32 = P.sbuf("tri32", [128, 128], F32)
        ones32, Bones32 = P.sbuf("ones32", [128, 128], F32)
        onesb, Bonesb = P.sbuf("onesb", [128, 128], BF16)
        bfb, Bbfb = P.sbuf("bfb_s", [128, 16], F32)
        nbfb, Bnbfb = P.sbuf("nbfb_s", [128, 16], F32)
        valid, Bvalid = P.sbuf("valid_s", [128, 8], F32)
        hmask, Bhmask = P.sbuf("hmask_s", [128, 1], F32)
        dneg, Bdneg = P.sbuf("dneg_s", [128, 1], F32)
        P.dma("pool", identb[:], ident_d, writes=[Bident])
        P.dma("pool", trib[:], tri_d, writes=[Btrib])
        P.dma("sp", tri32[:], tri_d, writes=[Btri32])
        P.dma("sp", bfb[:], bfb_d, writes=[Bbfb])
        P.dma("sp", valid[:], valid_d, writes=[Bvalid])
        P.dma("sp", hmask[:], hmask_d, writes=[Bhmask])
        P.dma("sp", dneg[:], dneg_d, writes=[Bdneg])
        P.op("pool", lambda e: e.memset(ones32[:], 1.0), writes=[Bones32])
        P.op("pool", lambda e: e.memset(onesb[:], 1.0), writes=[Bonesb])
        P.op("dve", lambda e: e.tensor_scalar(out=nbfb[:], in0=bfb[:], scalar1=-1.0, scalar2=None, op0=ALU.mult),
             reads=[Bbfb], writes=[Bnbfb])

        phA = stack.enter_context(ExitStack())
        WS.append(P.sbuf("ws3", [128, 16, 512], BF16, phA))
        LOGF, BLOGF = P.sbuf("LOGF", [128, NBLK, 16], F32, phA)
        MIXT, _ = P.sbuf("MIXT", [128, 16, NQ], BF16, phA)
        DALL, BDALL = P.sbuf("DALL", [128, NBLK, 16], F32, phA)
        BMIX = [P.buf(f"mix{m}") for m in range(16)]
        P.op("pool", lambda e: e.memset(LOGF[:], 0.0), writes=[BLOGF])

        try:
            rot = {"ps": 0, "ev": 0}

            psr = [0, 7]

            def nextps():
                i = psr[0] + rot["ps"] % (psr[1] - psr[0])
                rot["ps"] += 1
                return PS[i]

            def evac(out, in_, reads, writes, scale=None):
                i = rot["ev"]
                rot["ev"] += 1
                if i % 2 == 0:
                    if scale is None:
                        P.op("act", lambda e: e.copy(out=out, in_=in_), reads, writes)
                    else:
                        P.op("act", lambda e: e.mul(out, in_, scale), reads, writes)
                else:
                    if scale is None:
                        P.op("dve", lambda e: e.tensor_copy(out=out, in_=in_), reads, writes)
                    else:
                        P.op("dve", lambda e: e.tensor_scalar(out=out, in0=in_, scalar1=scale, scalar2=None,
                                                              op0=ALU.mult), reads, writes)

            def logf_chain(psz, nrow, dst, Bps, tmpz, Btmp):
                P.op("dve", lambda e: e.tensor_tensor(out=tmpz[0:nrow, :], in0=psz, in1=bfb[0:nrow, :], op=ALU.add),
                     reads=[Bps, Bbfb], writes=[Btmp])
                P.op("act", lambda e: e.activation(out=tmpz[0:nrow, :], in_=tmpz[0:nrow, :], func=AF.Exp, scale=-1.0),
                     reads=[Btmp], writes=[Btmp])
                P.op("act", lambda e: e.activation(out=tmpz[0:nrow, :], in_=tmpz[0:nrow, :], func=AF.Ln, bias=1.0, scale=1.0),
                     reads=[Btmp], writes=[Btmp])
                P.op("dve", lambda e: e.tensor_scalar(out=dst, in0=tmpz[0:nrow, :], scalar1=-1.0, scalar2=None,
                                                      op0=ALU.mult), reads=[Btmp], writes=[BLOGF])

            with ExitStack() as ph:
                Wz, BWz = P.sbuf("Wz", [128, 16, 16], BF16, ph)
                xT, BxT = P.sbuf("xT", [128, 16, NT], BF16, ph)
                P.dma("pool", WS[0][0][:], kp(w_in_d[:, 3072:3584]), writes=[WS[0][1]])

                def rest_of_phase1_weights():
                    P.dma("pool", WS[1][0][:], kp(w_in_d[:, 3584:4096]), writes=[WS[1][1]])
                    for g in range(2):
                        P.dma("pool", WS[2 + g][0][:], kp(w_in_d[:, 4096 + g * 512:4096 + (g + 1) * 512]),
                              writes=[WS[2 + g][1]])
                    P.dma("pool", Wz[:], kp(w_in_d[:, 5120:5136]), writes=[BWz])

                def wk(kc, pr):
                    return WS[pr // 4][0][:, kc, (pr % 4) * 128:(pr % 4 + 1) * 128]

                def wv(kc, cg):
                    return WS[2 + cg][0][:, kc, :]
                BWk = None

                with ExitStack() as ph1:
                    XJ = [P.sbuf(f"XJ{i}", [128, 16, 512], BF16, ph1) for i in range(2)]
                    KSTG = [P.sbuf(f"KSTG{i}", [128, 512], BF16, ph1) for i in range(2)]
                    KF32 = [P.sbuf(f"KF32{i}", [128, 512], F32, ph1) for i in range(2)]
                    VSTG = [P.sbuf(f"VSTG{i}", [128, 16, 4, 65], BF16, ph1) for i in range(2)]
                    VF32 = [P.sbuf("VF320", [128, 1024], F32, ph1)] * 2
                    TZ = [P.sbuf(f"TZ{i}", [128, 16], F32, ph1) for i in range(2)]
                    for i in range(2):
                        P.op("pool", lambda e, i=i: e.memset(VSTG[i][0][:], 1.0), writes=[VSTG[i][1]])
                    kcnt = 0
                    for s in range(16):
                        own = s >= 14
                        xj, Bxj = XJ[s % 2]
                        src = xT_d[:, (s - 14) * 512:(s - 13) * 512] if own else xrT_d[:, s * 512:(s + 1) * 512]
                        P.dma("pool", xj[:], kp(src), writes=[Bxj])
                        if s == 0:
                            rest_of_phase1_weights()
                        if s == 2:
                            for g in range(2):
                                P.dma("pool", xT[:, :, g * 544:(g + 1) * 544], kp(xT_d[:, g * 544:(g + 1) * 544]),
                                      writes=[BxT])
                        kcol = (KC_OWN + (s - 14) * 512) if own else s * 512
                        CK(1 + s)
                        for pr in range(8):
                            ps, Bps = nextps()
                            mm(P, ps[:, :512], [(wk(kc, pr), xj[:, kc, :]) for kc in range(16)],
                               [WS[pr // 4][1], Bxj], [Bps])
                            kst, Bkst = KSTG[kcnt % 2]
                            evac(kst[:], ps[:, :512], [Bps], [Bkst])
                            P.dma("sp", ktr_d[pr * 128:(pr + 1) * 128, kcol:kcol + 512], kst[:], reads=[Bkst],
                                  writes=[SCR["ktr"]], sembuf=Bkst)
                            if own:
                                kf, Bkf = KF32[kcnt % 2]
                                evac(kf[:], ps[:, :512], [Bps], [Bkf])
                                P.dma("sp", kT_o[pr * 128:(pr + 1) * 128, (s - 14) * 512:(s - 13) * 512], kf[:],
                                      reads=[Bkf], writes=[OUTB["kTo"]], sembuf=Bkf)
                            kcnt += 1
                        vst, Bvst = VSTG[s % 2]
                        for tb in range(4):
                            blk = s * 4 + tb
                            vf, Bvf = VF32[tb % 2]
                            for cg in range(2):
                                ps, Bps = nextps()
                                mm(P, ps[:, :512],
                                   [(xj[:, kc, tb * 128:(tb + 1) * 128], wv(kc, cg))
                                    for kc in range(16)], [WS[2 + cg][1], Bxj], [Bps])
                                evac(vst[:, cg * 8:(cg + 1) * 8, tb, 0:64],
                                     ps[:, :512].rearrange("p (h d) -> p h d", d=64), [Bps], [Bvst])
                                if own:
                                    evac(vf[:, cg * 512:(cg + 1) * 512], ps[:, :512], [Bps], [Bvf])
                            if own:
                                r0 = (s - 14) * 512 + tb * 128
                                P.dma("sp", v_o[r0:r0 + 128, :], vf[:], reads=[Bvf], writes=[OUTB["vo"]], sembuf=Bvf)
                            ps, Bps = nextps()
                            mm(P, ps[:, :16], [(xj[:, kc, tb * 128:(tb + 1) * 128], Wz[:, kc, :]) for kc in range(16)],
                               [BWz, Bxj], [Bps])
                            tz, Btz = TZ[tb % 2]
                            logf_chain(ps[:, :16], 128, LOGF[:, blk, :], Bps, tz, Btz)
                            if own:
                                r0 = (s - 14) * 512 + tb * 128
                                P.dma("sp", lf_o[r0:r0 + 128, :], LOGF[:, blk, :], reads=[BLOGF], writes=[OUTB["lfo"]],
                                      sembuf=OUTB["lfo"])
                        P.dma("sp", vr_d[:, :, s * 4:(s + 1) * 4, :].rearrange("h p b e -> p h b e"), vst[:],
                              reads=[Bvst], writes=[SCR["vr"]], sembuf=Bvst)
                P.barrier()

                CK(20)
                with ExitStack() as ph2:
                    KS2 = [P.sbuf(f"KS2{i}", [128, 34], BF16, ph2) for i in range(2)]
                    KF2 = [P.sbuf(f"KF2{i}", [128, 34], F32, ph2) for i in range(2)]
                    VS2, BVS2 = P.sbuf("VS2", [128, 16, 65], BF16, ph2)
                    VS3, BVS3 = P.sbuf("VS3", [128, 16, 65], BF16, ph2)
                    VF2, BVF2 = P.sbuf("VF2", [32, 1024], F32, ph2)
                    TZ2, BTZ2 = P.sbuf("TZ2", [128, 16], F32, ph2)
                    P.op("pool", lambda e: e.memset(VS2[:], 1.0), writes=[BVS2])
                    P.op("pool", lambda e: e.memset(VS3[:], 1.0), writes=[BVS3])
                    for pr in range(8):
                        ps, Bps = nextps()
                        mm(P, ps[:, :34], [(wk(kc, pr), xT[:, kc, HALO0:HALO0 + 34])
                                           for kc in range(16)], [WS[pr // 4][1], BxT], [Bps])
                        kst, Bkst = KS2[pr % 2]
                        kf, Bkf = KF2[pr % 2]
                        evac(kst[:], ps[:, :34], [Bps], [Bkst])
                        evac(kf[:], ps[:, :34], [Bps], [Bkf])
                        P.dma("sp", ktr_d[pr * 128:(pr + 1) * 128, KC_HALO:KC_HALO + 34], kst[:], reads=[Bkst],
                              writes=[SCR["ktr"]], sembuf=Bkst)
                        P.dma("sp", kT_o[pr * 128:(pr + 1) * 128, HALO0:HALO0 + 34], kf[:], reads=[Bkf],
                              writes=[OUTB["kTo"]], sembuf=Bkf)
                    for (c0, nrow, vs, Bvs, blk) in [(HALO0, 2, VS2, BVS2, B_HALO), (SMP0, 32, VS3, BVS3, B_SMP)]:
                        for cg in range(2):
                            ps, Bps = nextps()
                            mm(P, ps[0:nrow, :512], [(xT[:, kc, c0:c0 + nrow], wv(kc, cg))
                                                     for kc in range(16)], [WS[2 + cg][1], BxT], [Bps])
                            evac(vs[0:nrow, cg * 8:(cg + 1) * 8, 0:64],
                                 ps[0:nrow, :512].rearrange("p (h d) -> p h d", d=64), [Bps], [Bvs])
                            evac(VF2[0:nrow, cg * 512:(cg + 1) * 512], ps[0:nrow, :512], [Bps], [BVF2])
                        P.dma("sp", vr_d[:, :, blk, :].rearrange("h p e -> p h e"), vs[:, :, :], reads=[Bvs],
                              writes=[SCR["vr"]], sembuf=Bvs)
                        P.dma("sp", v_o[c0:c0 + nrow, :], VF2[0:nrow, :], reads=[BVF2], writes=[OUTB["vo"]], sembuf=BVF2)
                        ps, Bps = nextps()
                        mm(P, ps[0:nrow, :16], [(xT[:, kc, c0:c0 + nrow], Wz[:, kc, :]) for kc in range(16)],
                           [BWz, BxT], [Bps])
                        logf_chain(ps[0:nrow, :16], nrow, LOGF[0:nrow, blk, :], Bps, TZ2, BTZ2)
                        P.dma("sp", lf_o[c0:c0 + nrow, :], LOGF[0:nrow, blk, :], reads=[BLOGF], writes=[OUTB["lfo"]],
                              sembuf=OUTB["lfo"])

                    CK(21)
                    WQ1, BWQ1 = P.sbuf("WQ1", [128, 16, 512], BF16, ph2)
                    P.dma("pool", WQ1[:], kp(w_in_d[:, 2048:2560]), writes=[BWQ1])
                    for g in range(2):
                        P.dma("pool", WS[g][0][:], kp(w_in_d[:, g * 512:(g + 1) * 512]), writes=[WS[g][1]])
                        P.dma("pool", WS[2 + g][0][:], kp(w_in_d[:, 1024 + g * 512:1024 + (g + 1) * 512]),
                              writes=[WS[2 + g][1]])
                    WQ = [(WQ1, BWQ1)] * 2
                    WQA = [P.sbuf(f"WQA{i}", [128, 16, 128], BF16, ph2) for i in range(2)]
                    QSTG = [P.sbuf(f"QSTG{i}", [128, NQ], BF16, ph2) for i in range(2)]
                    FT = [P.sbuf(f"FT{i}", [128, NQ], F32, ph2) for i in range(2)]
                    NF = [P.sbuf(f"NF{i}", [128, NQ], F32, ph2) for i in range(2)]
                    ZERO, BZERO = P.sbuf("ZERO", [128, 1024], BF16, ph2)
                    P.op("pool", lambda e: e.memset(ZERO[:], 0.0), writes=[BZERO])
                    for i in range(2):
                        P.op("pool", lambda e, i=i: e.memset(WQA[i][0][:], 0.0), writes=[WQA[i][1]])
                    NTILES = [(0, 512), (512, 512), (1024, 34)]
                    for h in range(16):
                        if h == 8:
                            P.dma("pool", WQ1[:], kp(w_in_d[:, 2048 + (h // 8) * 512:2048 + (h // 8 + 1) * 512]),
                                  writes=[BWQ1])
                        par = h % 2
                        wqa, Bwqa = WQA[par]
                        wq, Bwq = WQ[h // 8]
                        off = (h % 8) * 64
                        dlo = 0 if par == 0 else 64
                        flo = 64 if par == 0 else 0
                        P.op("pool", lambda e: e.tensor_copy(out=wqa[:, :, dlo:dlo + 64], in_=wq[:, :, off:off + 64]),
                             reads=[Bwq], writes=[Bwqa])
                        P.op("pool", lambda e: e.tensor_copy(out=wqa[:, :, flo:flo + 1], in_=Wz[:, :, h:h + 1]),
                             reads=[BWz], writes=[Bwqa])
                        P.op("pool", lambda e: e.tensor_copy(out=wqa[:, :, flo + 32:flo + 33], in_=Wz[:, :, h:h + 1]),
                             reads=[BWz], writes=[Bwqa])
                        qs, Bqs = QSTG[par]
                        ft, Bft = FT[par]
                        nf, Bnf = NF[par]
                        R = slice(flo, flo + 64)
                        for (c0, n) in NTILES:
                            ps, Bps = nextps()
                            mm(P, ps[:, :n], [(wqa[:, kc, :], xT[:, kc, c0:c0 + n]) for kc in range(16)],
                               [Bwqa, BxT], [Bps])
                            evac(qs[dlo:dlo + 64, c0:c0 + n], ps[dlo:dlo + 64, :n], [Bps], [Bqs], scale=0.125)
                            P.op("act", lambda e, ps=ps, c0=c0, n=n: e.activation(
                                out=ft[R, c0:c0 + n], in_=ps[R, :n], func=AF.Exp, scale=-1.0, bias=nbfb[R, h:h + 1]),
                                reads=[Bps, Bnbfb], writes=[Bft])
                        P.op("act", lambda e: e.activation(out=ft[R, :], in_=ft[R, :], func=AF.Ln, bias=1.0, scale=1.0),
                             reads=[Bft], writes=[Bft])
                        for (c0, n) in [(HALO0, 2), (SMP0, 32), (0, 1024)]:
                            P.op("dve", lambda e, c0=c0, n=n: e.tensor_tensor_scan(
                                out=nf[R, c0:c0 + n], data0=ft[R, c0:c0 + n], data1=ZERO[R, 0:n], initial=0.0,
                                op0=ALU.add, op1=ALU.add), reads=[Bft, BZERO], writes=[Bnf])
                        P.op("dve", lambda e: e.tensor_scalar(out=nf[R, 0:1024], in0=nf[R, 0:1024],
                                                              scalar1=nf[R, HALO0 + 1:HALO0 + 2], scalar2=None,
                                                              op0=ALU.add), reads=[Bnf], writes=[Bnf])
                        P.op("dve", lambda e: e.tensor_scalar(out=qs[R, :], in0=nf[R, :], scalar1=-1.0, scalar2=None,
                                                              op0=ALU.mult), reads=[Bnf], writes=[Bqs])
                        R2 = slice(flo + 32, flo + 64)
                        P.op("dve", lambda e: e.scalar_tensor_tensor(out=qs[R2, :], in0=nf[R2, :], scalar=-1.0,
                                                                     in1=qs[R2, :], op0=ALU.mult, op1=ALU.subtract),
                             reads=[Bnf, Bqs], writes=[Bqs])
                        P.dma("sp", qtr_d[h], qs[:], reads=[Bqs], writes=[SCR["qtr"]], sembuf=Bqs)
                    P.barrier()
                CK(30)
                with ExitStack() as ph3:
                    CIN, _ = P.sbuf("CIN", [128, 8, 1118], F32, ph3)
                    BCIN = [P.buf(f"cin{m}") for m in range(8)]
                    wc, Bwc = P.sbuf("wc", [128, 8, 31], F32, ph3)
                    cv3, Bcv3 = P.sbuf("cv3", [128, 8, 3], F32, ph3)
                    P.dma("sp", wc[:], wconvT_d, writes=[Bwc])
                    P.dma("sp", cv3[:], cvec_d, writes=[Bcv3])
                    for m in range(8):
                        P.dma("sp", CIN[:, m, 1056:1086], ccT_d[m * 128:(m + 1) * 128, :], writes=[BCIN[m]])
                    with ExitStack() as ph3a:
                        CK(40)
                        P.dma("sp", LOGF[:, B_CACHE:B_CACHE + 16, :], clf_d.rearrange("(b p) h -> p b h", p=128), writes=[BLOGF])
                        P.deferred = []
                        psr[:] = [5, 7]
                        with ExitStack() as ph4x:
                            ph4 = ph3a
                            CB, BCB = P.sbuf("CB", [128, NBLK, 16], F32, ph4)
                            BT, BBT = P.sbuf("BT", [128, NBLK, 16], F32, ph4)
                            RP, BRP = P.sbuf("RP", [128, 11, 16], F32, ph4)
                            NEGV, BNEGV = P.sbuf("NEGV", [128, 8], F32, ph4)
                            P.op("dve", lambda e: e.memset(CB[:], 0.0), writes=[BCB])
                            P.op("dve", lambda e: e.memset(RP[:], 0.0), writes=[BRP])
                            P.op("dve", lambda e: e.tensor_scalar(out=NEGV[:], in0=valid[:], scalar1=-NEG, scalar2=NEG,
                                                                  op0=ALU.mult, op1=ALU.add), reads=[Bvalid], writes=[BNEGV])
                            for (b0, nb, nrow) in [(0, 28, 128), (28, 28, 128), (56, 8, 128), (B_HALO, 1, 2), (B_SMP, 1, 32),
                                                   (B_CACHE, 16, 128)]:
                                ps1, Bps1 = nextps()
                                rhs = LOGF[0:nrow, b0:b0 + nb, :].rearrange("p b h -> p (b h)")
                                mm(P, ps1[:, :nb * 16], [(tri32[0:nrow, :], rhs)], [Btri32, BLOGF], [Bps1])
                                P.op("act", lambda e: e.copy(out=CB[0:nrow, b0:b0 + nb, :],
                                                             in_=ps1[0:nrow, :nb * 16].rearrange("p (b h) -> p b h", h=16)),
                                     reads=[Bps1], writes=[BCB])
                                ps2, Bps2 = nextps()
                                mm(P, ps2[:, :nb * 16], [(ones32[0:nrow, :], rhs)], [Bones32, BLOGF], [Bps2])
                                P.op("act", lambda e: e.copy(out=BT[:, b0:b0 + nb, :],
                                                             in_=ps2[:, :nb * 16].rearrange("p (b h) -> p b h", h=16)),
                                     reads=[Bps2], writes=[BBT])
                            chunks = [(8 * j, j) for j in range(7)] + [(B_OWN, 7), (B_CACHE, 8), (B_CACHE + 8, 9)]
                            for (b0, ci) in chunks:
                                for i in range(8):
                                    if i > 0:
                                        P.op("dve", lambda e, i=i: e.tensor_tensor(out=CB[:, b0 + i, :], in0=CB[:, b0 + i, :],
                                                                                   in1=RP[:, ci, :], op=ALU.add),
                                             reads=[BCB, BRP], writes=[BCB])
                                    P.op("dve", lambda e, i=i: e.tensor_tensor(out=RP[:, ci, :], in0=RP[:, ci, :],
                                                                               in1=BT[:, b0 + i, :], op=ALU.add),
                                         reads=[BRP, BBT], writes=[BRP])
                            RN, BRN = P.sbuf("RN", [128, 8, 16], F32, ph4)
                            P.op("dve", lambda e: e.memset(RN[:], 0.0), writes=[BRN])
                            for j in range(6, -1, -1):
                                P.op("dve", lambda e: e.tensor_tensor(out=RN[:, j, :], in0=RP[:, j, :], in1=RN[:, j + 1, :],
                                                                      op=ALU.add), reads=[BRP, BRN], writes=[BRN])
                                P.op("dve", lambda e: e.tensor_scalar(out=RN[:, j, :], in0=RN[:, j, :], scalar1=valid[:, j:j + 1],
                                                                      scalar2=None, op0=ALU.mult), reads=[BRN, Bvalid], writes=[BRN])
                            for j in range(7):
                                P.op("dve", lambda e: e.tensor_scalar(out=RN[:, j, :], in0=RN[:, j, :], scalar1=NEGV[:, j:j + 1],
                                                                      scalar2=None, op0=ALU.add), reads=[BRN, BNEGV], writes=[BRN])
                                for i in range(8):
                                    b = 8 * j + i
                                    P.op("dve", lambda e, b=b: e.scalar_tensor_tensor(out=DALL[:, b, :], in0=CB[:, b, :], scalar=-1.0,
                                                                                      in1=RN[:, j, :], op0=ALU.mult, op1=ALU.add),
                                         reads=[BCB, BRN], writes=[BDALL])
                            P.op("dve", lambda e: e.tensor_scalar(out=DALL[:, 0, :], in0=DALL[:, 0, :], scalar1=dneg[:, 0:1],
                                                                  scalar2=None, op0=ALU.add), reads=[BDALL, Bdneg], writes=[BDALL])
                            for i in range(8):
                                b = B_OWN + i
                                P.op("dve", lambda e, b=b: e.scalar_tensor_tensor(out=DALL[:, b, :], in0=CB[:, b, :], scalar=-1.0,
                                                                                  in1=BT[:, B_HALO, :], op0=ALU.mult,
                                                                                  op1=ALU.subtract),
                                     reads=[BCB, BBT], writes=[BDALL])
                            for (b, nrow) in [(B_HALO, 2), (B_SMP, 32)]:
                                P.op("dve", lambda e, b=b, nrow=nrow: e.tensor_scalar(out=DALL[0:nrow, b, :], in0=CB[0:nrow, b, :],
                                                                                      scalar1=-1.0, scalar2=None, op0=ALU.mult),
                                     reads=[BCB], writes=[BDALL])
                            HNEG, BHNEG = P.sbuf("HNEG", [128, 1], F32, ph4)
                            P.op("dve", lambda e: e.tensor_scalar(out=HNEG[:], in0=hmask[:], scalar1=-NEG, scalar2=NEG,
                                                                  op0=ALU.mult, op1=ALU.add), reads=[Bhmask], writes=[BHNEG])
                            P.op("dve", lambda e: e.tensor_scalar(out=DALL[0:2, B_HALO, :], in0=DALL[0:2, B_HALO, :],
                                                                  scalar1=HNEG[0:2, 0:1], scalar2=None, op0=ALU.add),
                                 reads=[BDALL, BHNEG], writes=[BDALL])
                            P.op("dve", lambda e: e.tensor_tensor(out=RP[:, 8, :], in0=RP[:, 8, :], in1=RP[:, 9, :], op=ALU.add),
                                 reads=[BRP], writes=[BRP])
                            for ci in range(2):
                                for i in range(8):
                                    b = B_CACHE + 8 * ci + i
                                    P.op("dve", lambda e, b=b, ci=ci: e.scalar_tensor_tensor(
                                        out=DALL[:, b, :], in0=CB[:, b, :], scalar=-1.0, in1=RP[:, 8 + ci, :], op0=ALU.mult,
                                        op1=ALU.add), reads=[BCB, BRP], writes=[BDALL])
                            pass

                        P.defq = P.deferred
                        P.deferred = None
                        psr[:] = [0, 5]

                        SG = [P.sbuf(f"SG{i}", [128, 512], F32, ph3a) for i in range(2)]
                        SEGS = {0: [(0, 512, 32)], 512: [(0, 512, 544)], 1024: [(0, 2, 30), (2, 32, 1086), (34, 30, 0)]}
                        cnt = 0
                        for g in range(2):
                            ta, Bta = WS[g]
                            tg, Btg = WS[2 + g]
                            for m4 in range(4):
                                m = g * 4 + m4
                                for (c0, n) in [(0, 512), (512, 512), (1024, 64)]:
                                    psA, BpsA = nextps()
                                    mm(P, psA[:, :n], [(ta[:, kc, m4 * 128:(m4 + 1) * 128], xT[:, kc, c0:c0 + n])
                                                       for kc in range(16)], [Bta, BxT], [BpsA])
                                    psG, BpsG = nextps()
                                    mm(P, psG[:, :n], [(tg[:, kc, m4 * 128:(m4 + 1) * 128], xT[:, kc, c0:c0 + n])
                                                       for kc in range(16)], [Btg, BxT], [BpsG])
                                    sg, Bsg = SG[cnt % 2]
                                    cnt += 1
                                    P.flush(12)
                                    P.op("act", lambda e: e.activation(out=sg[:, :n], in_=psG[:, :n], func=AF.Sigmoid),
                                         reads=[BpsG], writes=[Bsg])
                                    for (s0, ln, d0) in SEGS[c0]:
                                        P.op("dve", lambda e, s0=s0, ln=ln, d0=d0: e.tensor_tensor(
                                            out=CIN[:, m, d0:d0 + ln], in0=psA[:, s0:s0 + ln], in1=sg[:, s0:s0 + ln],
                                            op=ALU.mult), reads=[BpsA, Bsg], writes=[BCIN[m]])
                        P.barrier()
                        psr[:] = [0, 7]
                    P.dma("sp", convT_o[:, 0:30].rearrange("(m p) t -> p m t", p=128), CIN[:, :, 1026:1056],
                          reads=BCIN, writes=[OUTB["convTo"]], sembuf=OUTB["convTo"])
                    P.dma("sp", convT_o[:, 30:60].rearrange("(m p) t -> p m t", p=128), CIN[:, :, 1088:1118],
                          reads=BCIN, writes=[OUTB["convTo"]], sembuf=OUTB["convTo"])
                    CK(31)
                    with ExitStack() as ph3t:
                        ACC2 = [P.sbuf(f"ACCD{i}", [128, 1088], F32, ph3t) for i in range(2)]
                        for m in range(8):
                            a0, Ba0 = ACC2[0]
                            a1, Ba1 = ACC2[1]
                            P.op("dve", lambda e: e.tensor_scalar(out=a0[:], in0=CIN[:, m, 0:1088], scalar1=wc[:, m, 0:1],
                                                                  scalar2=cv3[:, m, 0:1], op0=ALU.mult, op1=ALU.add),
                                 reads=[BCIN[m], Bwc, Bcv3], writes=[Ba0])
                            P.op("dve", lambda e: e.tensor_scalar(out=a1[:], in0=CIN[:, m, 1:1089], scalar1=wc[:, m, 1:2],
                                                                  scalar2=None, op0=ALU.mult),
                                 reads=[BCIN[m], Bwc], writes=[Ba1])
                            for j in range(2, 31):
                                ac, Bac = ACC2[j % 2]
                                P.op("dve", lambda e, j=j, ac=ac: e.scalar_tensor_tensor(
                                    out=ac[:], in0=CIN[:, m, j:j + 1088], scalar=wc[:, m, j:j + 1], in1=ac[:],
                                    op0=ALU.mult, op1=ALU.add), reads=[BCIN[m], Bac], writes=[Bac])
                            P.op("dve", lambda e: e.tensor_tensor(out=a0[:], in0=a0[:], in1=a1[:], op=ALU.add),
                                 reads=[Ba0, Ba1], writes=[Ba0])
                            P.op("act", lambda e: e.copy(out=CIN[:, m, 30:1118], in_=a0[:]), reads=[Ba0],
                                 writes=[BCIN[m]])
                        P.barrier()
                    with ExitStack() as ph3b:
                        MEAN, BMEAN = P.sbuf("MEAN", [128, 1088], F32, ph3b)
                        RSTD, BRSTD = P.sbuf("RSTD", [128, 1088], F32, ph3b)
                        SQ = [P.sbuf(f"SQ{i}", [128, 512], F32, ph3b) for i in range(2)]
                        TMPV, BTMPV = P.sbuf("TMPV", [128, 512], F32, ph3b)
                        CK(32)
                        for (o0, n) in [(0, 512), (512, 512), (1024, 64)]:
                            psS, BpsS = nextps()
                            mm(P, psS[:, :n], [(ones32[:, :], CIN[:, m, 30 + o0:30 + o0 + n]) for m in range(8)],
                               BCIN + [Bones32], [BpsS])
                            psQ, BpsQ = nextps()
                            for m in range(8):
                                sq, Bsq = SQ[m % 2]
                                P.op("act", lambda e: e.activation(out=sq[:, :n], in_=CIN[:, m, 30 + o0:30 + o0 + n],
                                                                   func=AF.Square), reads=[BCIN[m]], writes=[Bsq])
                                P.op("pe", lambda e, m=m: e.matmul(psQ[:, :n], lhsT=ones32[:, :], rhs=sq[:, :n],
                                                                   start=(m == 0), stop=(m == 7)),
                                     reads=[Bsq, Bones32], writes=[BpsQ])
                            P.op("act", lambda e: e.mul(MEAN[:, o0:o0 + n], psS[:, :n], 1.0 / 1024), reads=[BpsS],
                                 writes=[BMEAN])
                            P.op("dve", lambda e: e.tensor_tensor(out=TMPV[:, :n], in0=MEAN[:, o0:o0 + n],
                                                                  in1=MEAN[:, o0:o0 + n], op=ALU.mult),
                                 reads=[BMEAN], writes=[BTMPV])
                            P.op("dve", lambda e: e.scalar_tensor_tensor(out=RSTD[:, o0:o0 + n], in0=psQ[:, :n],
                                                                         scalar=1.0 / 1024, in1=TMPV[:, :n],
                                                                         op0=ALU.mult, op1=ALU.subtract),
                                 reads=[BpsQ, BTMPV], writes=[BRSTD])
                            P.op("act", lambda e: e.activation(out=RSTD[:, o0:o0 + n], in_=RSTD[:, o0:o0 + n], func=AF.Ln,
                                                               bias=LN_EPS, scale=1.0), reads=[BRSTD], writes=[BRSTD])
                            P.op("act", lambda e: e.activation(out=RSTD[:, o0:o0 + n], in_=RSTD[:, o0:o0 + n], func=AF.Exp,
                                                               scale=-0.5), reads=[BRSTD], writes=[BRSTD])
                        for m in range(8):
                            cm = CIN[:, m, 30:1118]
                            P.op("dve", lambda e: e.tensor_tensor(out=cm, in0=cm, in1=MEAN[:], op=ALU.subtract),
                                 reads=[BCIN[m], BMEAN], writes=[BCIN[m]])
                            P.op("dve", lambda e: e.tensor_tensor(out=cm, in0=cm, in1=RSTD[:], op=ALU.mult),
                                 reads=[BCIN[m], BRSTD], writes=[BCIN[m]])
                            for (s0, ln, d0) in [(2, 1024, 0), (0, 2, HALO0), (1056, 32, SMP0)]:
                                P.op("act", lambda e, s0=s0, ln=ln, d0=d0: e.activation(
                                    out=MIXT[:, m, d0:d0 + ln], in_=CIN[:, m, 30 + s0:30 + s0 + ln], func=AF.Silu,
                                    scale=cv3[:, m, 1:2], bias=cv3[:, m, 2:3]), reads=[BCIN[m], Bcv3], writes=[BMIX[m]])
                        P.barrier()
                P.barrier()

            P.barrier()

            WSTREAM = [(wd, g) for wd in (w_out_d, w_mk_d, w_mv_d, w_mq_d, w_mo_d) for g in range(4)]
            wsi = [0]

            def ws_issue():
                i = wsi[0]
                if i < len(WSTREAM):
                    wd, g = WSTREAM[i]
                    P.dma("pool", WS[i % 4][0][:], kp(wd[:, g * 512:(g + 1) * 512]), writes=[WS[i % 4][1]])
                    wsi[0] += 1
            for _ in range(4):
                ws_issue()

            CK(50)
            with ExitStack() as ph5:
                KALL = [P.sbuf(f"KALL{i}", [128, NKEY], BF16, ph5) for i in range(2)]
                VH = [P.sbuf(f"VH{i}", [128, NBLK, 65], BF16, ph5) for i in range(2)]
                QH = [P.sbuf(f"QH{i}", [128, NQ], BF16, ph5) for i in range(2)]
                PT = [P.sbuf(f"PT{i}", [128, 512], BF16, ph5) for i in range(8)]
                OS = [P.sbuf(f"OS{i}", [65, 512], F32, ph5) for i in range(4)]
                ONB = [P.sbuf(f"ONB{i}", [64, 512], BF16, ph5) for i in range(2)]
                shiftb, Bshiftb = P.sbuf("shiftb", [64, 128], BF16, ph5)
                P.dma("pool", shiftb[:], shift_d, writes=[Bshiftb])
                P.op("dve", lambda e: e.memset(VH[0][0][:], 1.0), writes=[VH[0][1]])
                P.op("dve", lambda e: e.memset(VH[1][0][:], 1.0), writes=[VH[1][1]])
                for par in range(2):
                    k_, Bk_ = KALL[par]
                    P.op("dve", lambda e: e.memset(k_[:], 0.0), writes=[Bk_])
                    f0 = 64 if par == 0 else 0
                    P.op("dve", lambda e: e.memset(k_[f0:f0 + 1, :], 1.0), writes=[Bk_])
                    P.op("dve", lambda e: e.memset(k_[f0 + 32:f0 + 33, :], 1.0), writes=[Bk_])
                PSS = [PS[i] for i in range(0, 4)]
                PSO = [PS[4], PS[5], PS[6], PS[7]]
                BKC = [P.buf("kallc0"), P.buf("kallc1")]
                BVC = [P.buf("vhc0"), P.buf("vhc1")]
                scnt = [0]
                ocnt = [0]
                pcnt = [0]
                pending = []

                def att_jobs(h, par, jobs):
                    kal, Bkal = KALL[par]
                    vh, Bvh = VH[h % 2]
                    qh, Bqh = QH[h % 2]
                    kreads = [[Bkal] + ([BKC[par]] if any(b[0] >= B_CACHE for b in j[2]) else []) for j in jobs]
                    vreads = [[Bvh] + ([BVC[h % 2]] if any(b[0] >= B_CACHE for b in j[2]) else []) for j in jobs]
                    LA = 3
                    pi = pcnt[0]
                    pcnt[0] += 1
                    st = []
                    for ji, (q0, nq, blocks, mixcols) in enumerate(jobs):
                        st.append({"pso": PSO[(pi % 2) * 2 + ji], "slots": []})

                    def stage_a(ji, bi):
                        q0, nq, blocks, mixcols = jobs[ji]
                        blk, nk, kcol, qlo, mask = blocks[bi]
                        pss, Bpss = PSS[scnt[0] % 4]
                        scnt[0] += 1
                        pt, Bpt = PT[ocnt[0] % len(PT)]
                        ocnt[0] += 1
                        st[ji]["slots"].append((pt, Bpt))
                        w = nq - qlo
                        P.op("pe", lambda e: e.matmul(pss[0:nk, 0:w], lhsT=kal[:, kcol:kcol + nk],
                                                      rhs=qh[:, q0 + qlo:q0 + nq], start=True, stop=True),
                             reads=kreads[ji] + [Bqh], writes=[Bpss])
                        P.op("act", lambda e: e.activation(out=pt[0:nk, 0:w], in_=pss[0:nk, 0:w], func=AF.Exp,
                                                           bias=DALL[0:nk, blk, h:h + 1], scale=1.0),
                             reads=[Bpss, BDALL], writes=[Bpt])
                        if mask:
                            P.op("dve", lambda e: e.tensor_tensor(out=pt[0:nk, 0:nk], in0=pt[0:nk, 0:nk],
                                                                   in1=trib[0:nk, 0:nk], op=ALU.mult),
                                 reads=[Bpt, Btrib], writes=[Bpt])

                    def stage_b(ji, bi):
                        q0, nq, blocks, mixcols = jobs[ji]
                        blk, nk, kcol, qlo, mask = blocks[bi]
                        pt, Bpt = st[ji]["slots"][bi]
                        pso, Bpso = st[ji]["pso"]
                        w = nq - qlo
                        nb = len(blocks)
                        P.op("pe", lambda e: e.matmul(pso[0:65, qlo:nq], lhsT=vh[0:nk, blk, :], rhs=pt[0:nk, 0:w],
                                                      start=(bi == 0), stop=(bi == nb - 1)),
                             reads=vreads[ji] + [Bpt], writes=[Bpso])
                    maxnb = max(len(j[2]) for j in jobs)
                    for i in range(maxnb + LA):
                        for ji in range(len(jobs)):
                            if i < len(jobs[ji][2]):
                                stage_a(ji, i)
                        for ji in range(len(jobs)):
                            if 0 <= i - LA < len(jobs[ji][2]):
                                stage_b(ji, i - LA)
                        if i >= 2 and pending:
                            pending.pop(0)()
                    while pending:
                        pending.pop(0)()
                    for ji, (q0, nq, blocks, mixcols) in enumerate(jobs):
                        pso, Bpso = st[ji]["pso"]
                        osb, Bosb = OS[(pi % 2) * 2 + ji]
                        onb, Bonb = ONB[pi % 2]
                        ch = 8 + h // 2

                        def t1(pso=pso, Bpso=Bpso, osb=osb, Bosb=Bosb, nq=nq):
                            P.op("dve", lambda e: e.tensor_copy(out=osb[:, :nq], in_=pso[0:65, :nq]), reads=[Bpso],
                                 writes=[Bosb])

                        def t2(osb=osb, Bosb=Bosb, nq=nq):
                            P.op("dve", lambda e: e.tensor_scalar(out=osb[64:65, :nq], in0=osb[64:65, :nq],
                                                                  scalar1=1e-30, scalar2=None, op0=ALU.add),
                                 reads=[Bosb], writes=[Bosb])

                        def t3(osb=osb, Bosb=Bosb, nq=nq):
                            P.op("dve", lambda e: e.reciprocal(out=osb[64:65, :nq], in_=osb[64:65, :nq]), reads=[Bosb],
                                 writes=[Bosb])

                        def t4(pso=pso, Bpso=Bpso, osb=osb, Bosb=Bosb, nq=nq):
                            P.op("pe", lambda e: e.matmul(pso[:, :nq], lhsT=ones32[64:65, :], rhs=osb[64:65, :nq],
                                                          start=True, stop=True), reads=[Bosb, Bones32], writes=[Bpso])
                        pending.extend([t1, t2, t3, t4])
                        if par == 0:
                            def t5(pso=pso, Bpso=Bpso, osb=osb, Bosb=Bosb, nq=nq, mixcols=mixcols, ch=ch):
                                P.op("dve", lambda e: e.tensor_tensor(out=MIXT[0:64, ch, mixcols:mixcols + nq],
                                                                      in0=osb[0:64, :nq], in1=pso[0:64, :nq],
                                                                      op=ALU.mult),
                                     reads=[Bosb, Bpso], writes=[BMIX[ch]])
                            pending.append(t5)
                        else:
                            def t5(pso=pso, Bpso=Bpso, osb=osb, Bosb=Bosb, nq=nq, onb=onb, Bonb=Bonb):
                                P.op("dve", lambda e: e.tensor_tensor(out=onb[:, :nq], in0=osb[0:64, :nq],
                                                                      in1=pso[0:64, :nq], op=ALU.mult),
                                     reads=[Bosb, Bpso], writes=[Bonb])

                            def t6(pso=pso, Bpso=Bpso, nq=nq, onb=onb, Bonb=Bonb):
                                P.op("pe", lambda e: e.matmul(pso[:, :nq], lhsT=shiftb[:, :], rhs=onb[:, :nq],
                                                              start=True, stop=True), reads=[Bonb, Bshiftb],
                                     writes=[Bpso])

                            def t7(pso=pso, Bpso=Bpso, nq=nq, mixcols=mixcols, ch=ch):
                                P.op("dve", lambda e: e.tensor_copy(out=MIXT[64:128, ch, mixcols:mixcols + nq],
                                                                    in_=pso[64:128, :nq]), reads=[Bpso],
                                     writes=[BMIX[ch]])
                            pending.extend([t5, t6, t7])

                REM = [(b, 128, 128 * b, 0, False) for b in range(56)]
                HAL = (B_HALO, 2, KC_HALO, 0, False)
                for h in range(16):
                    CK(51 + h)
                    par = h % 2
                    dlo = 0 if par == 0 else 64
                    kal, Bkal = KALL[par]
                    vh, Bvh = VH[h % 2]
                    qh, Bqh = QH[h % 2]
                    P.dma("sp", kal[dlo:dlo + 64, 0:KC_CACHE], ktr_d[h * 64:(h + 1) * 64, :], reads=[SCR["ktr"]],
                          writes=[Bkal])
                    P.dma("pool", kal[dlo:dlo + 64, KC_CACHE:NKEY], ckT_d[h * 64:(h + 1) * 64, :], reads=[Bkal], writes=[BKC[par]])
                    P.dma("sp", vh[:, 0:B_CACHE, :], vr_d[h], reads=[SCR["vr"]], writes=[Bvh])
                    P.dma("pool", vh[:, B_CACHE:NBLK, 0:64],
                          cv_d[:, h * 64:(h + 1) * 64].rearrange("(b p) d -> p b d", p=128), reads=[Bvh],
                          writes=[BVC[h % 2]])
                    P.dma("sp", qh[:], qtr_d[h], reads=[SCR["qtr"]], writes=[Bqh])
                    blocks0 = REM + [HAL] + [(B_OWN + kb, 128, KC_OWN + 128 * kb, 128 * kb, True) for kb in range(4)]
                    blocks1 = REM + [HAL] + [(B_OWN + kb, 128, KC_OWN + 128 * kb, 0, False) for kb in range(4)] + \
                        [(B_OWN + kb, 128, KC_OWN + 128 * kb, 128 * (kb - 4), True) for kb in range(4, 8)]
                    blocks_h = REM + [(B_HALO, 2, KC_HALO, 0, True)]
                    blocks_s = [(B_CACHE + b, 128, KC_CACHE + 128 * b, 0, False) for b in range(16)] + \
                        [(B_SMP, 32, KC_SMP, 0, True)]
                    att_jobs(h, par, [(0, 512, blocks0, 0), (HALO0, 2, blocks_h, HALO0)])
                    att_jobs(h, par, [(512, 512, blocks1, 512), (SMP0, 32, blocks_s, SMP0)])
                while pending:
                    pending.pop(0)()
                P.barrier()
            P.barrier()
            def load_dd(wd):
                for g in range(4):
                    P.dma("pool", WS[g][0][:], kp(wd[:, g * 512:(g + 1) * 512]), writes=[WS[g][1]])

            TBLK = [(tb * 128, 128) for tb in range(8)] + [(1024, 34)]

            def proj_ln(srcT, Bsrc, res_d, lnidx, dstT, BdstT, out_d, out_buf, scope, prefix, wtiles=None,
                        zsrc_d=None):
                Z = [P.sbuf(f"{prefix}Z{i}", [128, D], F32, scope) for i in range(3)]
                ZB = [P.sbuf(f"{prefix}ZB{i}", [128, D], BF16, scope) for i in range(2)]
                LNP, BLNP = P.sbuf(f"{prefix}LNP", [128, 2, D], F32, scope)
                ST = [P.sbuf(f"{prefix}ST{i}", [128, 4, 6], F32, scope) for i in range(2)]
                MV = [P.sbuf(f"{prefix}MV{i}", [128, 4], F32, scope) for i in range(2)]
                P.dma("sp", LNP[:, 0, :], lnp_d[lnidx], writes=[BLNP])
                P.dma("sp", LNP[:, 1, :], lnp_d[lnidx + 1], writes=[BLNP])
                nblk = len(TBLK)

                def load(ti):
                    c0, nt = TBLK[ti]
                    z, Bz = Z[ti % 3]
                    P.dma("sp", z[0:nt, :], zsrc_d[c0:c0 + nt, :], reads=[BZSCR], writes=[Bz])

                def s1a(ti):
                    c0, nt = TBLK[ti]
                    z, Bz = Z[ti % 3]
                    st, Bst = ST[ti % 2]
                    mv, Bmv = MV[ti % 2]
                    for g in range(4):
                        P.op("dve", lambda e, g=g: e.bn_stats(out=st[0:nt, g, :], in_=z[0:nt, g * 512:(g + 1) * 512]),
                             reads=[Bz], writes=[Bst])
                    P.op("dve", lambda e: e.bn_aggr(out=mv[0:nt, 0:2], in_=st[0:nt, :, :].rearrange("p a b -> p (a b)")),
                         reads=[Bst], writes=[Bmv])

                def s1b(ti):
                    c0, nt = TBLK[ti]
                    mv, Bmv = MV[ti % 2]
                    P.op("act", lambda e: e.activation(out=mv[0:nt, 1:2], in_=mv[0:nt, 1:2], func=AF.Ln, bias=LN_EPS,
                                                       scale=1.0), reads=[Bmv], writes=[Bmv])
                    P.op("act", lambda e: e.activation(out=mv[0:nt, 1:2], in_=mv[0:nt, 1:2], func=AF.Exp, scale=-0.5),
                         reads=[Bmv], writes=[Bmv])

                def s1c(ti):
                    c0, nt = TBLK[ti]
                    mv, Bmv = MV[ti % 2]
                    P.op("dve", lambda e: e.scalar_tensor_tensor(out=mv[0:nt, 2:3], in0=mv[0:nt, 0:1], scalar=-1.0,
                                                                 in1=mv[0:nt, 1:2], op0=ALU.mult, op1=ALU.mult),
                         reads=[Bmv], writes=[Bmv])

                def s2a1(ti):
                    c0, nt = TBLK[ti]
                    z, Bz = Z[ti % 3]
                    mv, Bmv = MV[ti % 2]
                    P.op("act", lambda e: e.activation(out=z[0:nt, :], in_=z[0:nt, :], func=AF.Identity,
                                                       scale=mv[0:nt, 1:2], bias=mv[0:nt, 2:3]),
                         reads=[Bz, Bmv], writes=[Bz])

                def s2a2(ti):
                    c0, nt = TBLK[ti]
                    z, Bz = Z[ti % 3]
                    P.op("dve", lambda e: e.tensor_tensor(out=z[0:nt, :], in0=z[0:nt, :], in1=LNP[0:nt, 0, :],
                                                          op=ALU.mult), reads=[Bz, BLNP], writes=[Bz])
                    P.op("dve", lambda e: e.tensor_tensor(out=z[0:nt, :], in0=z[0:nt, :], in1=LNP[0:nt, 1, :],
                                                          op=ALU.add), reads=[Bz, BLNP], writes=[Bz])

                def s2a3(ti):
                    c0, nt = TBLK[ti]
                    z, Bz = Z[ti % 3]
                    P.dma("sp", out_d[c0:c0 + nt, :], z[0:nt, :], reads=[Bz], writes=[out_buf], sembuf=Bz)
                    if dstT is not None:
                        zb, Bzb = ZB[ti % 2]
                        P.op("act", lambda e: e.copy(out=zb[0:nt, :], in_=z[0:nt, :]), reads=[Bz], writes=[Bzb])

                def s2b(ti):
                    if dstT is None:
                        return
                    c0, nt = TBLK[ti]
                    zb, Bzb = ZB[ti % 2]
                    for half in range(2):
                        for j in range(8):
                            kc = half * 8 + j
                            P.op("pe", lambda e, kc=kc, j=j: e.transpose(
                                out=PST[:, j * 128:j * 128 + nt], in_=zb[0:nt, kc * 128:(kc + 1) * 128],
                                identity=identb[0:nt, 0:nt]), reads=[Bzb, Bident], writes=[BPST])
                        evac(dstT[:, half * 8:(half + 1) * 8, c0:c0 + nt],
                             PST[:, :].rearrange("p (j t) -> p j t", t=128)[:, :, 0:nt], [BPST], BdstT)
                load(0)
                load(1)
                s1a(0)
                s1b(0)
                s1c(0)
                for ti in range(nblk):
                    if ti + 2 < nblk:
                        load(ti + 2)
                    s2a1(ti)
                    if ti + 1 < nblk:
                        s1a(ti + 1)
                        s1b(ti + 1)
                    s2a2(ti)
                    if ti + 1 < nblk:
                        s1c(ti + 1)
                    s2a3(ti)
                    if ti >= 1:
                        s2b(ti - 1)
                s2b(nblk - 1)

            x2t_d = dscr("x2t", [128, 16, NT], BF16)
            BX2T = P.buf("x2t")
            zscr_d = dscr("zscr", [9 * 128, D], F32)
            BZSCR = P.buf("zscr")
            def stream_proj(srcT, Bsrc, res_d, res_bufs, t0, scope, prefix):
                XB3 = [P.sbuf(f"{prefix}XB3{i}", [128, 512], F32, scope) for i in range(3)]
                Z3 = [P.sbuf(f"{prefix}Z3{i}", [128, 512], F32, scope) for i in range(3)]
                steps = [(g, ti) for g in range(4) for ti in range(len(TBLK))]

                def ld(si):
                    g_, ti_ = steps[si]
                    c0_, nt_ = TBLK[ti_]
                    xb_, Bxb_ = XB3[si % 3]
                    P.dma("sp", xb_[0:nt_, :], res_d[c0_:c0_ + nt_, g_ * 512:(g_ + 1) * 512], reads=res_bufs,
                          writes=[Bxb_])
                ld(0)
                for g in range(4):
                    wt, Bwt = WS[(t0 + g) % 4]
                    for ti, (c0, nt) in enumerate(TBLK):
                        si = g * len(TBLK) + ti
                        if si + 1 < len(steps):
                            ld(si + 1)
                        xb, Bxb = XB3[si % 3]
                        z3, Bz3 = Z3[si % 3]
                        ps, Bps = nextps()
                        mm(P, ps[0:nt, :512], [(srcT[:, kc, c0:c0 + nt], wt[:, kc, :]) for kc in range(16)],
                           [Bwt] + Bsrc, [Bps])
                        P.op("dve", lambda e: e.scalar_tensor_tensor(out=z3[0:nt, :], in0=xb[0:nt, :], scalar=ALPHA,
                                                                     in1=ps[0:nt, :512], op0=ALU.mult, op1=ALU.add),
                             reads=[Bxb, Bps], writes=[Bz3])
                        P.dma("sp", zscr_d[c0:c0 + nt, g * 512:(g + 1) * 512], z3[0:nt, :], reads=[Bz3],
                              writes=[BZSCR], sembuf=Bz3)
                    ws_issue()

            with ExitStack() as ph34:
                XT, BXT = P.sbuf("XT", [128, 16, NT], BF16, ph34)
                P.op("pool", lambda e: e.memset(XT[:], 0.0), writes=[BXT])
                CK(70)
                with ExitStack() as ph6:
                    stream_proj(MIXT, BMIX, xtm_d, [], 0, ph6, "a")
                    proj_ln(None, None, None, 0, XT, [BXT], xres_d, SCR["xres"], ph6, "a", zsrc_d=zscr_d)
                    P.barrier()
                P.barrier()

                CK(80)
                with ExitStack() as ph7:
                    MKT, BMKT = P.sbuf("MKT", [128, 16, 512], BF16, ph7)
                    MVT, BMVT = P.sbuf("MVT", [128, 4, D], BF16, ph7)
                    with ExitStack() as ph7a:
                        memT, BmemT = P.sbuf("memT", [128, 16, 256], BF16, ph7a)
                        MF = [P.sbuf(f"MF{i}", [128, 512], F32, ph7a) for i in range(2)]
                        P.dma("pool", memT[:], kp(memT_d), writes=[BmemT])
                        P.dma("pool", MKT[:, :, 256:512], kp(cmkT_d), writes=[BMKT])
                        P.dma("pool", MVT[:, 2:4, :], cmv_d.rearrange("(b p) n -> p b n", p=128), writes=[BMVT])
                        for (t0, od, obuf, is_k) in [(4, memk_o, OUTB["memk"], True), (8, memv_o, OUTB["memv"], False)]:
                            for g in range(4):
                                wt, Bwt = WS[(t0 + g) % 4]
                                for blk in range(2):
                                    ps, Bps = nextps()
                                    mm(P, ps[:, :512], [(memT[:, kc, blk * 128:(blk + 1) * 128], wt[:, kc, :])
                                                        for kc in range(16)], [Bwt, BmemT], [Bps])
                                    mf, Bmf = MF[blk % 2]
                                    evac(mf[:], ps[:, :512], [Bps], [Bmf])
                                    P.dma("sp", od[blk * 128:(blk + 1) * 128, g * 512:(g + 1) * 512], mf[:], reads=[Bmf],
                                          writes=[obuf], sembuf=Bmf)
                                    if not is_k:
                                        evac(MVT[:, blk, g * 512:(g + 1) * 512], ps[:, :512], [Bps], [BMVT])
                                if is_k:
                                    for c in range(4):
                                        m = 4 * g + c
                                        ps, Bps = nextps()
                                        mm(P, ps[:, :256], [(wt[:, kc, c * 128:(c + 1) * 128], memT[:, kc, :])
                                                            for kc in range(16)], [Bwt, BmemT], [Bps])
                                        evac(MKT[:, m, 0:256], ps[:, :256], [Bps], [BMKT])
                                ws_issue()
                        P.barrier()
                    CK(81)
                    QMT = [P.sbuf(f"QMT{i}", [128, 4, NQ], BF16, ph7) for i in range(2)]
                    PT2 = [P.sbuf(f"PT2{i}", [128, 2, 512], BF16, ph7) for i in range(2)]
                    RD = [P.sbuf(f"RD{i}", [128, 512], F32, ph7) for i in range(2)]
                    tcnt = 0
                    for hm in range(4):
                        qmt, Bqmt = QMT[hm % 2]
                        wt, Bwt = WS[(12 + hm) % 4]
                        for c in range(4):
                            for (c0, n) in [(0, 512), (512, 512), (1024, 34)]:
                                ps, Bps = nextps()
                                mm(P, ps[:, :n], [(wt[:, kc, c * 128:(c + 1) * 128], XT[:, kc, c0:c0 + n])
                                                  for kc in range(16)], [Bwt, BXT], [Bps])
                                evac(qmt[:, c, c0:c0 + n], ps[:, :n], [Bps], [Bqmt], scale=512.0 ** -0.5)
                        ws_issue()
                        for (c0, n, mc0, vb0) in [(0, 512, 0, 0), (512, 512, 0, 0), (HALO0, 2, 0, 0), (SMP0, 32, 256, 2)]:
                            pt, Bpt = PT2[tcnt % 2]
                            rd, Brd = RD[tcnt % 2]
                            tcnt += 1
                            for mb in range(2):
                                ps, Bps = nextps()
                                mm(P, ps[:, :n], [(MKT[:, 4 * hm + c, mc0 + mb * 128:mc0 + (mb + 1) * 128],
                                                   qmt[:, c, c0:c0 + n]) for c in range(4)], [BMKT, Bqmt], [Bps])
                                P.op("act", lambda e, mb=mb: e.activation(out=pt[:, mb, :n], in_=ps[:, :n], func=AF.Exp),
                                     reads=[Bps], writes=[Bpt])
                            psd, Bpsd = nextps()
                            mm(P, psd[:, :n], [(onesb[:, :], pt[:, mb, :n]) for mb in range(2)], [Bonesb, Bpt], [Bpsd])
                            P.op("dve", lambda e: e.reciprocal(out=rd[:, :n], in_=psd[:, :n]), reads=[Bpsd], writes=[Brd])
                            for c in range(4):
                                ch = 4 * hm + c
                                ps, Bps = nextps()
                                mm(P, ps[:, :n], [(MVT[:, vb0 + mb, ch * 128:(ch + 1) * 128], pt[:, mb, :n])
                                                  for mb in range(2)], [BMVT, Bpt], [Bps])
                                P.op("dve", lambda e, ch=ch: e.tensor_tensor(out=MIXT[:, ch, c0:c0 + n], in0=ps[:, :n],
                                                                             in1=rd[:, :n], op=ALU.mult),
                                     reads=[Bps, Brd], writes=[BMIX[ch]])
                    P.barrier()
                P.barrier()
                CK(85)
                with ExitStack() as ph8:
                    stream_proj(MIXT, BMIX, xres_d, [SCR["xres"]], 16, ph8, "b")
                    for (slot, col0) in [(0, 0), (1, DFF), (2, 512)]:
                        P.dma("pool", WS[slot][0][:], kp(w_up_d[:, col0:col0 + 512]), writes=[WS[slot][1]])
                    proj_ln(None, None, None, 2, XT, [BXT], xres_d, SCR["xres"], ph8, "b", zsrc_d=zscr_d)
                    P.dma("sp", x2t_d, XT[:], reads=[BXT], writes=[BX2T], sembuf=BX2T)
                    P.barrier()
                P.barrier()
            P.barrier()
            phA.close()
            P.barrier()

            CK(90)
            with ExitStack() as ph9:
                ACTT, BACTT = P.sbuf("ACTT", [128, 44, NQ], BF16, ph9)
                wff, Bwff = P.sbuf("wff", [128, 88, 3], F32, ph9)
                bff, Bbff = P.sbuf("bff", [128, 88], F32, ph9)
                CF, BCF = P.sbuf("CF", [128, 88, 2], F32, ph9)
                HOUT, BHOUT = P.sbuf("HOUT", [128, 88, 4], F32, ph9)
                P.dma("sp", wff[:], wffnT_d, writes=[Bwff])
                P.dma("sp", bff[:], bffn_d, writes=[Bbff])
                P.dma("sp", CF[:], cffT_d.rearrange("(c p) t -> p c t", p=128), writes=[BCF])
                with ExitStack() as ph9a:
                    X2, BX2 = P.sbuf("X2", [128, 16, NT], BF16, ph9a)
                    P.dma("sp", X2[:], x2t_d, reads=[BX2T], writes=[BX2])
                    HB = [P.sbuf(f"HB{i}", [128, 1060], F32, ph9a) for i in range(2)]
                    CB2 = [P.sbuf(f"CB2{i}", [128, NQ], F32, ph9a) for i in range(2)]
                    wi = [0]

                    def wload(col0):
                        t, Bt = WS[wi[0] % 3]
                        if wi[0] >= 3:
                            P.dma("pool", t[:], kp(w_up_d[:, col0:col0 + 512]), writes=[Bt])
                        wi[0] += 1
                        return t, Bt
                    for t in range(11):
                        tgw = wload(t * 512)
                        tvw = wload(DFF + t * 512)
                        for m4 in range(4):
                            i = t * 4 + m4
                            for gi, (wt, Bwt) in enumerate([tgw, tvw]):
                                ci = gi * 44 + i
                                hb, Bhb = HB[gi]
                                cb, Bcb = CB2[gi]
                                P.op("act", lambda e: e.copy(out=hb[:, 1026:1028], in_=CF[:, ci, :]), reads=[BCF],
                                     writes=[Bhb])
                                for (c0, n) in [(0, 512), (512, 512), (1024, 34)]:
                                    ps, Bps = nextps()
                                    mm(P, ps[:, :n], [(wt[:, kc, m4 * 128:(m4 + 1) * 128], X2[:, kc, c0:c0 + n])
                                                      for kc in range(16)], [Bwt, BX2], [Bps])
                                    if c0 < 1024:
                                        evac(hb[:, 2 + c0:2 + c0 + n], ps[:, :n], [Bps], [Bhb])
                                    else:
                                        P.op("dve", lambda e: e.tensor_scalar(out=hb[:, 0:2], in0=ps[:, 0:2],
                                                                              scalar1=hmask[:, 0:1], scalar2=None,
                                                                              op0=ALU.mult),
                                             reads=[Bps, Bhmask], writes=[Bhb])
                                        evac(hb[:, 1028:1060], ps[:, 2:34], [Bps], [Bhb])
                                P.op("act", lambda e: e.copy(out=HOUT[:, ci, 0:2], in_=hb[:, 1024:1026]), reads=[Bhb],
                                     writes=[BHOUT])
                                P.op("act", lambda e: e.copy(out=HOUT[:, ci, 2:4], in_=hb[:, 1058:1060]), reads=[Bhb],
                                     writes=[BHOUT])
                                P.op("act", lambda e: e.activation(out=cb[:, :], in_=hb[:, 2:1060], func=AF.Identity,
                                                                   scale=wff[:, ci, 2:3], bias=bff[:, ci:ci + 1]),
                                     reads=[Bhb, Bwff, Bbff], writes=[Bcb])
                                P.op("dve", lambda e: e.scalar_tensor_tensor(out=cb[:, :], in0=hb[:, 1:1059],
                                                                             scalar=wff[:, ci, 1:2], in1=cb[:, :],
                                                                             op0=ALU.mult, op1=ALU.add),
                                     reads=[Bhb, Bwff, Bcb], writes=[Bcb])
                                P.op("dve", lambda e: e.scalar_tensor_tensor(out=cb[:, :], in0=hb[:, 0:1058],
                                                                             scalar=wff[:, ci, 0:1], in1=cb[:, :],
                                                                             op0=ALU.mult, op1=ALU.add),
                                     reads=[Bhb, Bwff, Bcb], writes=[Bcb])
                            cg_, Bcg_ = CB2[0]
                            cv_, Bcv_ = CB2[1]
                            P.op("act", lambda e: e.activation(out=cg_[:, :], in_=cg_[:, :], func=AF.Silu),
                                 reads=[Bcg_], writes=[Bcg_])
                            P.op("dve", lambda e: e.tensor_tensor(out=ACTT[:, i, :], in0=cg_[:, :], in1=cv_[:, :],
                                                                  op=ALU.mult), reads=[Bcg_, Bcv_], writes=[BACTT])
                    P.dma("sp", ffnT_o.rearrange("(c p) t -> p c t", p=128), HOUT[:], reads=[BHOUT],
                          writes=[OUTB["ffnTo"]], sembuf=BHOUT)
                    KT = [(0, 16), (16, 16), (32, 12)]
                    WD0 = [WS[1], WS[2], WS[0]]
                    for kt, (k0, nk) in enumerate(KT):
                        P.dma("pool", WD0[kt][0][:, 0:nk, :],
                              w_down_d[k0 * 128:(k0 + nk) * 128, 0:512].rearrange("(kc p) n -> p kc n", p=128),
                              writes=[WD0[kt][1]])
                    P.barrier()
                P.barrier()
                CK(95)
                with ExitStack() as ph9b:
                    XB3 = [P.sbuf(f"XB3{i}", [128, 512], F32, ph9b) for i in range(3)]
                    Z3 = [P.sbuf(f"Z3{i}", [128, 512], F32, ph9b) for i in range(3)]
                    WD = WD0 + [P.sbuf(f"WD{i}", [128, 16, 512], BF16, ph9b) for i in range(3)]

                    def wd_load(g):
                        for kt, (k0, nk) in enumerate(KT):
                            t_, Bt_ = WD[(g % 2) * 3 + kt]
                            P.dma("pool", t_[:, 0:nk, :],
                                  w_down_d[k0 * 128:(k0 + nk) * 128, g * 512:(g + 1) * 512].rearrange(
                                      "(kc p) n -> p kc n", p=128), writes=[Bt_])
                    steps = [(g, ti) for g in range(4) for ti in range(len(TBLK))]

                    def ld(si):
                        g_, ti_ = steps[si]
                        c0_, nt_ = TBLK[ti_]
                        xb_, Bxb_ = XB3[si % 3]
                        P.dma("sp", xb_[0:nt_, :], xres_d[c0_:c0_ + nt_, g_ * 512:(g_ + 1) * 512],
                              reads=[SCR["xres"]], writes=[Bxb_])
                    ld(0)
                    for g in range(4):
                        if g + 1 < 4:
                            wd_load(g + 1)
                        wset = WD[(g % 2) * 3:(g % 2) * 3 + 3]
                        for ti, (c0, nt) in enumerate(TBLK):
                            si = g * len(TBLK) + ti
                            if si + 1 < len(steps):
                                ld(si + 1)
                            xb, Bxb = XB3[si % 3]
                            z3, Bz3 = Z3[si % 3]
                            ps, Bps = nextps()
                            for kt, (k0, nk) in enumerate(KT):
                                def fk(e, kt=kt, k0=k0, nk=nk):
                                    for kc in range(nk):
                                        inst = e.matmul(ps[0:nt, :512], lhsT=ACTT[:, k0 + kc, c0:c0 + nt],
                                                        rhs=wset[kt][0][:, kc, :], start=(kt == 0 and kc == 0),
                                                        stop=(kt == 2 and kc == nk - 1))
                                    return inst
                                P.op("pe", fk, [BACTT, wset[kt][1]], [Bps])
                            P.op("dve", lambda e: e.scalar_tensor_tensor(out=z3[0:nt, :], in0=xb[0:nt, :], scalar=ALPHA,
                                                                         in1=ps[0:nt, :512], op0=ALU.mult, op1=ALU.add),
                                 reads=[Bxb, Bps], writes=[Bz3])
                            P.dma("sp", zscr_d[c0:c0 + nt, g * 512:(g + 1) * 512], z3[0:nt, :], reads=[Bz3],
                                  writes=[BZSCR], sembuf=Bz3)
                    P.barrier()
                P.barrier()
            P.barrier()
            CK(97)
            with ExitStack() as ph10:
                proj_ln(None, None, None, 4, None, None, y_d, OUTB["y"], ph10, "c", zsrc_d=zscr_d)
                P.barrier()

        except StopBuild:
            pass
        P.finish(list(OUTB.values()) + list(SCR.values()))
    return nc


_NC_CACHE = {}


def _prep_inputs(inp):
    f = lambda a: np.ascontiguousarray(np.asarray(a, dtype=np.float32))
    xp = f(inp["x_prompt"])[0]
    xsm = f(inp["x_sample"])
    z2 = np.zeros((2, D), np.float32)
    xr = np.concatenate([z2, xp[:NREM - 2]], 0)
    xrT = np.ascontiguousarray(xr.T)
    shared = {
        "xrT": xrT,
        "w_in": f(inp["w_in"])[0], "w_out": f(inp["w_out"])[0], "w_mq": f(inp["w_mq"])[0],
        "w_mk": f(inp["w_mk"])[0], "w_mv": f(inp["w_mv"])[0], "w_mo": f(inp["w_mo"])[0],
        "w_up": f(inp["w_up"])[0], "w_down": f(inp["w_down"])[0],
        "wconvT": np.ascontiguousarray(f(inp["w_conv"])[0].T.reshape(8, 128, 31).transpose(1, 0, 2)),
        "cvec": np.ascontiguousarray(np.stack([f(inp["b_conv"])[0], f(inp["g_conv_ln"])[0], f(inp["b_conv_ln"])[0]],
                                              -1).reshape(8, 128, 3).transpose(1, 0, 2)),
        "wffnT": np.ascontiguousarray(f(inp["w_ffn_conv"])[0].T.reshape(88, 128, 3).transpose(1, 0, 2)),
        "bffn": np.ascontiguousarray(f(inp["b_ffn_conv"])[0].reshape(88, 128).T),
        "lnp": np.ascontiguousarray(np.broadcast_to(
            np.stack([f(inp[k])[0] for k in ["g_ln1", "b_ln1", "g_ln2", "b_ln2", "g_ln3", "b_ln3"]])[:, None, :],
            (6, 128, D))),
        "bfb": np.ascontiguousarray(np.broadcast_to(f(inp["b_forget"])[0][None, :], (128, 16))),
        "ident": np.eye(128, dtype=np.float32),
        "tri": np.triu(np.ones((128, 128), np.float32)),
        "dneg": np.concatenate([np.full((2, 1), NEG, np.float32), np.zeros((126, 1), np.float32)], 0),
        "memT": np.ascontiguousarray(f(inp["mem_prompt"])[0].T),
        "shift": np.eye(128, dtype=np.float32)[64:128].T[0:64].copy() if False else np.concatenate([np.zeros((64, 64), np.float32), np.eye(64, dtype=np.float32)], 1),
    }
    maps = []
    for c in range(NCORE):
        own = xp[OWN * c:OWN * (c + 1)]
        if c > 0:
            halo = xp[OWN * c - 2:OWN * c]
            ch = xp[OWN * c - 32:OWN * c - 2]
        else:
            halo = np.zeros((2, D), np.float32)
            ch = np.zeros((30, D), np.float32)
        xloc = np.concatenate([own, halo, xsm[c], ch], 0)
        m = dict(shared)
        m["xtm"] = np.ascontiguousarray(xloc)
        m["xT"] = np.ascontiguousarray(xloc.T)
        m["valid"] = np.ascontiguousarray(np.broadcast_to((np.arange(8) < c).astype(np.float32)[None, :], (128, 8)))
        m["hmask"] = np.full((128, 1), 0.0 if c == 0 else 1.0, np.float32)
        m["ckT"] = np.ascontiguousarray(f(inp["cache_fox_k"])[0, c].reshape(2048, 1024).T)
        m["cv"] = np.ascontiguousarray(f(inp["cache_fox_v"])[0, c].reshape(2048, 1024))
        m["clf"] = np.ascontiguousarray(f(inp["cache_fox_logf"])[0, c])
        m["ccT"] = np.ascontiguousarray(f(inp["cache_conv"])[0, c].T)
        m["cffT"] = np.ascontiguousarray(f(inp["cache_ffn"])[0, c].T)
        m["cmkT"] = np.ascontiguousarray(f(inp["cache_mem_k"])[0, c].reshape(256, D).T)
        m["cmv"] = np.ascontiguousarray(f(inp["cache_mem_v"])[0, c].reshape(256, D))
        maps.append(m)
    return maps


def _assemble(res):
    R = res
    yp = np.concatenate([R[c]["y"][:OWN] for c in range(NCORE)], 0)[None]
    ys = np.stack([R[c]["y"][SMP0:SMP0 + 32] for c in range(NCORE)], 0)
    p_conv = np.ascontiguousarray(R[7]["convTo"][:, 0:30].T)[None, None]
    pk = np.concatenate([R[c]["kTo"][:, :OWN].T for c in range(NCORE)], 0).reshape(1, 1, 8192, 16, 64)
    pv = np.concatenate([R[c]["vo"][:OWN] for c in range(NCORE)], 0).reshape(1, 1, 8192, 16, 64)
    plf = np.concatenate([R[c]["lfo"][:OWN] for c in range(NCORE)], 0).reshape(1, 1, 8192, 16)
    pmk = R[0]["memk"].reshape(1, 1, 256, 4, 512)
    pmv = R[0]["memv"].reshape(1, 1, 256, 4, 512)
    pf = np.ascontiguousarray(R[7]["ffnTo"][:, 0:2].T)[None, None]
    s_conv = np.stack([R[c]["convTo"][:, 30:60].T for c in range(NCORE)], 0)[None]
    sk = np.stack([R[c]["kTo"][:, SMP0:SMP0 + 32].T for c in range(NCORE)], 0).reshape(1, 8, 32, 16, 64)
    sv = np.stack([R[c]["vo"][SMP0:SMP0 + 32] for c in range(NCORE)], 0).reshape(1, 8, 32, 16, 64)
    slf = np.stack([R[c]["lfo"][SMP0:SMP0 + 32] for c in range(NCORE)], 0).reshape(1, 8, 32, 16)
    sf = np.stack([R[c]["ffnTo"][:, 2:4].T for c in range(NCORE)], 0)[None]
    outs = (yp, ys, p_conv, pk, pv, plf, pmk, pmv, pf, s_conv, sk, sv, slf, sf)
    return tuple(np.ascontiguousarray(o, dtype=np.float32) for o in outs)


def kernel(**inputs):
    if "nc" not in _NC_CACHE:
        _NC_CACHE["nc"] = build_nc()
    nc = _NC_CACHE["nc"]
    maps = _prep_inputs(inputs)
    res = run_bass_kernel_spmd(nc, maps, core_ids=list(range(NCORE)))
    return _assemble(res.results)
```

```python
import numpy as np
from contextlib import ExitStack
import concourse.bass as bass
import concourse.mybir as mybir
from concourse.bass_utils import run_bass_kernel_spmd

F32 = mybir.dt.float32
BF16 = mybir.dt.bfloat16
AF = mybir.ActivationFunctionType
ALU = mybir.AluOpType
EPOCH = 6000

D = 2048
NCORE = 8
OWN = 1024
NT = 1088
NQ = 1058
HALO0 = 1024
SMP0 = 1026
CH0 = 1058
NREM = 7168
NKEY = NREM + OWN + 2 + 32 + 2048
KC_OWN = NREM
KC_HALO = NREM + OWN
KC_SMP = KC_HALO + 2
KC_CACHE = KC_SMP + 32
NBLK = 56 + 8 + 1 + 1 + 16
B_OWN, B_HALO, B_SMP, B_CACHE = 56, 64, 65, 66
DFF = 5632
NEG = -30000.0
ALPHA = 2.0 ** 0.25
LN_EPS = 1e-5


class Eng:
    def __init__(self, name, h):
        self.name, self.h = name, h
        self.sems = []
        self.count = 0
        self.seen = {}


class Buf:
    def __init__(self, name, excl=False):
        self.name = name
        self.excl = excl
        self.w = {}
        self.r = {}
        self.dsem = None
        self.dcount = 0


class _Rec:
    def __init__(self):
        self.calls = []

    def __getattr__(self, name):
        def m(*a, **k):
            self.calls.append((name, a, k))
            return None
        return m


class Prog:
    def __init__(self, nc, stack):
        self.nc, self.stack = nc, stack
        self.eng = {}
        for name, h in [("pe", nc.tensor), ("act", nc.scalar), ("dve", nc.vector),
                        ("pool", nc.gpsimd), ("sp", nc.sync)]:
            self.eng[name] = Eng(name, h)
        self.allbufs = []
        self.sempool = {}
        self.deferred = None
        self.defq = []

    def newsem(self, name):
        return self.stack.enter_context(self.nc.semaphore(name))

    def buf(self, name, excl=False):
        b = Buf(name, excl)
        self.allbufs.append(b)
        return b

    def sbuf(self, name, shape, dt, stack=None):
        t = (stack or self.stack).enter_context(self.nc.sbuf_tensor("sb_" + name, shape, dt))
        return t, self.buf(name)

    def _wait(self, E, need):
        for key, (kind, ref, val) in need.items():
            if E.seen.get(key, 0) >= val:
                continue
            if kind == "e":
                src = self.eng[ref]
                ep, v = divmod(val - 1, EPOCH)
                E.h.wait_ge(src.sems[ep], v + 1)
            else:
                E.h.wait_ge(ref, val)
            E.seen[key] = val

    @staticmethod
    def _merge(need, d):
        for k, t in d.items():
            if k not in need or need[k][2] < t[2]:
                need[k] = t

    def _deps(self, reads, writes):
        need = {}
        for b in reads:
            self._merge(need, b.w)
            if b.excl:
                self._merge(need, b.r)
        for b in writes:
            self._merge(need, b.w)
            self._merge(need, b.r)
        return need

    def op(self, e, fn, reads=(), writes=()):
        if DEAD[0]:
            return None
        if self.deferred is not None:
            rec = _Rec()
            fn(rec)
            calls = rec.calls

            def replay(h, calls=calls):
                inst = None
                for (name, a, k) in calls:
                    inst = getattr(h, name)(*a, **k)
                return inst
            self.deferred.append((e, replay, list(reads), list(writes)))
            return None
        E = self.eng[e]
        self._wait(E, self._deps(reads, writes))
        inst = fn(E.h)
        ep, v = divmod(E.count, EPOCH)
        if ep >= len(E.sems):
            E.sems.append(self.newsem(f"s_{e}{ep}"))
        inst.then_inc(E.sems[ep], 1)
        E.count += 1
        tok = ("e", e, E.count)
        for b in reads:
            b.r[e] = tok
        for b in writes:
            b.w[e] = tok
        return inst

    def dma(self, q, out, in_, reads=(), writes=(), sembuf=None, **kw):
        if DEAD[0]:
            return None
        E = self.eng[q]
        self._wait(E, self._deps(reads, writes))
        sb = sembuf or (writes[0] if writes else reads[0])
        kind = "sw" if q == "pool" else "hw"
        if sb.dsem is None:
            sb.dsem = {}
        if kind not in sb.dsem:
            pool = self.sempool.setdefault(kind, [])
            if pool:
                sb.dsem[kind] = pool.pop()
            else:
                self.nsem = getattr(self, "nsem", 0) + 1
                nm = f"d{kind}{self.nsem}"
                sb.dsem[kind] = [self.newsem(nm), 0, nm]
        ent = sb.dsem[kind]
        inst = E.h.dma_start(out=out, in_=in_, **kw)
        inst.then_inc(ent[0], 16)
        ent[1] += 16
        key = ent[2]
        tok = ("d", ent[0], ent[1])
        for b in reads:
            b.r[key] = tok
        for b in writes:
            b.w[key] = tok
        return inst

    def flush(self, n=None):
        k = len(self.defq) if n is None else min(n, len(self.defq))
        for _ in range(k):
            e, fn, reads, writes = self.defq.pop(0)
            self.op(e, fn, reads, writes)

    def barrier(self):
        if DEAD[0]:
            return
        self.flush()
        need = {}
        for b in self.allbufs:
            self._merge(need, b.w)
            self._merge(need, b.r)
        for E in self.eng.values():
            self._wait(E, need)
        for b in self.allbufs:
            if b.dsem:
                for kind, ent in b.dsem.items():
                    self.sempool.setdefault(kind, []).append(ent)
                b.dsem = None

    def finish(self, bufs, e="sp"):
        DEAD[0] = False
        E = self.eng[e]
        need = {}
        for b in bufs:
            self._merge(need, b.w)
            self._merge(need, b.r)
        self._wait(E, need)


def mm(P, out, pairs, reads, writes):
    def f(e):
        n = len(pairs)
        for i, (l, r) in enumerate(pairs):
            inst = e.matmul(out, lhsT=l, rhs=r, start=(i == 0), stop=(i == n - 1))
        return inst
    return P.op("pe", f, reads, writes)


class StopBuild(Exception):
    pass


import os
KLIMIT = int(os.environ.get("KLIMIT", "999"))


DEAD = [False]


def CK(n):
    if n > KLIMIT:
        DEAD[0] = True


def kp(ap):
    return ap.rearrange("(kc p) n -> p kc n", p=128)


def build_nc(stage=99):
    DEAD[0] = False
    nc = bass.Bass("TRN2", target_bir_lowering=False)

    def din(name, shape, dt=F32):
        return nc.dram_tensor(name, list(shape), dt, kind="ExternalInput").ap()

    def dout(name, shape, dt=F32):
        return nc.dram_tensor(name, list(shape), dt, kind="ExternalOutput").ap()

    def dscr(name, shape, dt):
        return nc.dram_tensor(name, list(shape), dt).ap()

    xT_d = din("xT", [D, NT])
    xtm_d = din("xtm", [NT, D])
    xrT_d = din("xrT", [D, NREM])
    w_in_d = din("w_in", [D, 5136])
    w_out_d = din("w_out", [D, D])
    w_mq_d = din("w_mq", [D, D])
    w_mk_d = din("w_mk", [D, D])
    w_mv_d = din("w_mv", [D, D])
    w_mo_d = din("w_mo", [D, D])
    w_up_d = din("w_up", [D, 2 * DFF])
    w_down_d = din("w_down", [DFF, D])
    wconvT_d = din("wconvT", [128, 8, 31])
    cvec_d = din("cvec", [128, 8, 3])
    wffnT_d = din("wffnT", [128, 88, 3])
    bffn_d = din("bffn", [128, 88])
    lnp_d = din("lnp", [6, 128, D])
    bfb_d = din("bfb", [128, 16])
    valid_d = din("valid", [128, 8])
    hmask_d = din("hmask", [128, 1])
    ident_d = din("ident", [128, 128])
    tri_d = din("tri", [128, 128])
    dneg_d = din("dneg", [128, 1])
    shift_d = din("shift", [64, 128])
    ckT_d = din("ckT", [1024, 2048])
    cv_d = din("cv", [2048, 1024])
    clf_d = din("clf", [2048, 16])
    ccT_d = din("ccT", [1024, 30])
    cffT_d = din("cffT", [2 * DFF, 2])
    memT_d = din("memT", [D, 256])
    cmkT_d = din("cmkT", [D, 256])
    cmv_d = din("cmv", [256, D])

    y_d = dout("y", [NQ, D])
    kT_o = dout("kTo", [1024, NQ])
    v_o = dout("vo", [NQ, 1024])
    lf_o = dout("lfo", [NQ, 16])
    convT_o = dout("convTo", [1024, 60])
    memk_o = dout("memk", [256, D])
    memv_o = dout("memv", [256, D])
    ffnT_o = dout("ffnTo", [2 * DFF, 4])

    ktr_d = dscr("ktr", [1024, KC_CACHE], BF16)
    vr_d = dscr("vr", [16, 128, B_CACHE, 65], BF16)
    qtr_d = dscr("qtr", [16, 128, NQ], BF16)
    xres_d = dscr("xres", [9 * 128, D], F32)

    outs = []
    with ExitStack() as stack:
        P = Prog(nc, stack)
        OUTB = {n: P.buf(n) for n in ["y", "kTo", "vo", "lfo", "convTo", "memk", "memv", "ffnTo"]}
        SCR = {n: P.buf(n) for n in ["ktr", "vr", "qtr", "xres"]}
        PS = []
        for i in range(8):
            t = stack.enter_context(nc.psum_tensor(f"ps{i}", [128, 512], F32))
            PS.append((t, P.buf(f"ps{i}", excl=True)))
        PST = PS[7][0][:].bitcast(BF16)
        BPST = PS[7][1]
        WS = [P.sbuf(f"ws{i}", [128, 16, 512], BF16) for i in range(3)]

        identb, Bident = P.sbuf("identb", [128, 128], BF16)
        trib, Btrib = P.sbuf("trib", [128, 128], BF16)
        tri32, Btri32 = P.sbuf("tri32", [128, 128], F32)
        ones32, Bones32 = P.sbuf("ones32", [128, 128], F32)
        onesb, Bonesb = P.sbuf("onesb", [128, 128], BF16)
        bfb, Bbfb = P.sbuf("bfb_s", [128, 16], F32)
        nbfb, Bnbfb = P.sbuf("nbfb_s", [128, 16], F32)
        valid, Bvalid = P.sbuf("valid_s", [128, 8], F32)
        hmask, Bhmask = P.sbuf("hmask_s", [128, 1], F32)
        dneg, Bdneg = P.sbuf("dneg_s", [128, 1], F32)
        P.dma("pool", identb[:], ident_d, writes=[Bident])
        P.dma("pool", trib[:], tri_d, writes=[Btrib])
        P.dma("sp", tri32[:], tri_d, writes=[Btri32])
        P.dma("sp", bfb[:], bfb_d, writes=[Bbfb])
        P.dma("sp", valid[:], valid_d, writes=[Bvalid])
        P.dma("sp", hmask[:], hmask_d, writes=[Bhmask])
        P.dma("sp", dneg[:], dneg_d, writes=[Bdneg])
        P.op("pool", lambda e: e.memset(ones32[:], 1.0), writes=[Bones32])
        P.op("pool", lambda e: e.memset(onesb[:], 1.0), writes=[Bonesb])
        P.op("dve", lambda e: e.tensor_scalar(out=nbfb[:], in0=bfb[:], scalar1=-1.0, scalar2=None, op0=ALU.mult),
             reads=[Bbfb], writes=[Bnbfb])

        phA = stack.enter_context(ExitStack())
        WS.append(P.sbuf("ws3", [128, 16, 512], BF16, phA))
        LOGF, BLOGF = P.sbuf("LOGF", [128, NBLK, 16], F32, phA)
        MIXT, _ = P.sbuf("MIXT", [128, 16, NQ], BF16, phA)
        DALL, BDALL = P.sbuf("DALL", [128, NBLK, 16], F32, phA)
        BMIX = [P.buf(f"mix{m}") for m in range(16)]
        P.op("pool", lambda e: e.memset(LOGF[:], 0.0), writes=[BLOGF])

        try:
            rot = {"ps": 0, "ev": 0}

            psr = [0, 7]

            def nextps():
                i = psr[0] + rot["ps"] % (psr[1] - psr[0])
                rot["ps"] += 1
                return PS[i]

            def evac(out, in_, reads, writes, scale=None):
                i = rot["ev"]
                rot["ev"] += 1
                if i % 2 == 0:
                    if scale is None:
                        P.op("act", lambda e: e.copy(out=out, in_=in_), reads, writes)
                    else:
                        P.op("act", lambda e: e.mul(out, in_, scale), reads, writes)
                else:
                    if scale is None:
                        P.op("dve", lambda e: e.tensor_copy(out=out, in_=in_), reads, writes)
                    else:
                        P.op("dve", lambda e: e.tensor_scalar(out=out, in0=in_, scalar1=scale, scalar2=None,
                                                              op0=ALU.mult), reads, writes)

            def logf_chain(psz, nrow, dst, Bps, tmpz, Btmp):
                P.op("dve", lambda e: e.tensor_tensor(out=tmpz[0:nrow, :], in0=psz, in1=bfb[0:nrow, :], op=ALU.add),
                     reads=[Bps, Bbfb], writes=[Btmp])
                P.op("act", lambda e: e.activation(out=tmpz[0:nrow, :], in_=tmpz[0:nrow, :], func=AF.Exp, scale=-1.0),
                     reads=[Btmp], writes=[Btmp])
                P.op("act", lambda e: e.activation(out=tmpz[0:nrow, :], in_=tmpz[0:nrow, :], func=AF.Ln, bias=1.0, scale=1.0),
                     reads=[Btmp], writes=[Btmp])
                P.op("dve", lambda e: e.tensor_scalar(out=dst, in0=tmpz[0:nrow, :], scalar1=-1.0, scalar2=None,
                                                      op0=ALU.mult), reads=[Btmp], writes=[BLOGF])

            with ExitStack() as ph:
                Wz, BWz = P.sbuf("Wz", [128, 16, 16], BF16, ph)
                xT, BxT = P.sbuf("xT", [128, 16, NT], BF16, ph)
                P.dma("pool", WS[0][0][:], kp(w_in_d[:, 3072:3584]), writes=[WS[0][1]])

                def rest_of_phase1_weights():
                    P.dma("pool", WS[1][0][:], kp(w_in_d[:, 3584:4096]), writes=[WS[1][1]])
                    for g in range(2):
                        P.dma("pool", WS[2 + g][0][:], kp(w_in_d[:, 4096 + g * 512:4096 + (g + 1) * 512]),
                              writes=[WS[2 + g][1]])
                    P.dma("pool", Wz[:], kp(w_in_d[:, 5120:5136]), writes=[BWz])

                def wk(kc, pr):
                    return WS[pr // 4][0][:, kc, (pr % 4) * 128:(pr % 4 + 1) * 128]

                def wv(kc, cg):
                    return WS[2 + cg][0][:, kc, :]
                BWk = None

                with ExitStack() as ph1:
                    XJ = [P.sbuf(f"XJ{i}", [128, 16, 512], BF16, ph1) for i in range(2)]
                    KSTG = [P.sbuf(f"KSTG{i}", [128, 512], BF16, ph1) for i in range(2)]
                    KF32 = [P.sbuf(f"KF32{i}", [128, 512], F32, ph1) for i in range(2)]
                    VSTG = [P.sbuf(f"VSTG{i}", [128, 16, 4, 65], BF16, ph1) for i in range(2)]
                    VF32 = [P.sbuf("VF320", [128, 1024], F32, ph1)] * 2
                    TZ = [P.sbuf(f"TZ{i}", [128, 16], F32, ph1) for i in range(2)]
                    for i in range(2):
                        P.op("pool", lambda e, i=i: e.memset(VSTG[i][0][:], 1.0), writes=[VSTG[i][1]])
                    kcnt = 0
                    for s in range(16):
                        own = s >= 14
                        xj, Bxj = XJ[s % 2]
                        src = xT_d[:, (s - 14) * 512:(s - 13) * 512] if own else xrT_d[:, s * 512:(s + 1) * 512]
                        P.dma("pool", xj[:], kp(src), writes=[Bxj])
                        if s == 0:
                            rest_of_phase1_weights()
                        if s == 2:
                            for g in range(2):
                                P.dma("pool", xT[:, :, g * 544:(g + 1) * 544], kp(xT_d[:, g * 544:(g + 1) * 544]),
                                      writes=[BxT])
                        kcol = (KC_OWN + (s - 14) * 512) if own else s * 512
                        CK(1 + s)
                        for pr in range(8):
                            ps, Bps = nextps()
                            mm(P, ps[:, :512], [(wk(kc, pr), xj[:, kc, :]) for kc in range(16)],
                               [WS[pr // 4][1], Bxj], [Bps])
                            kst, Bkst = KSTG[kcnt % 2]
                            evac(kst[:], ps[:, :512], [Bps], [Bkst])
                            P.dma("sp", ktr_d[pr * 128:(pr + 1) * 128, kcol:kcol + 512], kst[:], reads=[Bkst],
                                  writes=[SCR["ktr"]], sembuf=Bkst)
                            if own:
                                kf, Bkf = KF32[kcnt % 2]
                                evac(kf[:], ps[:, :512], [Bps], [Bkf])
                                P.dma("sp", kT_o[pr * 128:(pr + 1) * 128, (s - 14) * 512:(s - 13) * 512], kf[:],
                                      reads=[Bkf], writes=[OUTB["kTo"]], sembuf=Bkf)
                            kcnt += 1
                        vst, Bvst = VSTG[s % 2]
                        for tb in range(4):
                            blk = s * 4 + tb
                            vf, Bvf = VF32[tb % 2]
                            for cg in range(2):
                                ps, Bps = nextps()
                                mm(P, ps[:, :512],
                                   [(xj[:, kc, tb * 128:(tb + 1) * 128], wv(kc, cg))
                                    for kc in range(16)], [WS[2 + cg][1], Bxj], [Bps])
                                evac(vst[:, cg * 8:(cg + 1) * 8, tb, 0:64],
                                     ps[:, :512].rearrange("p (h d) -> p h d", d=64), [Bps], [Bvst])
                                if own:
                                    evac(vf[:, cg * 512:(cg + 1) * 512], ps[:, :512], [Bps], [Bvf])
                            if own:
                                r0 = (s - 14) * 512 + tb * 128
                                P.dma("sp", v_o[r0:r0 + 128, :], vf[:], reads=[Bvf], writes=[OUTB["vo"]], sembuf=Bvf)
                            ps, Bps = nextps()
                            mm(P, ps[:, :16], [(xj[:, kc, tb * 128:(tb + 1) * 128], Wz[:, kc, :]) for kc in range(16)],
                               [BWz, Bxj], [Bps])
                            tz, Btz = TZ[tb % 2]
                            logf_chain(ps[:, :16], 128, LOGF[:, blk, :], Bps, tz, Btz)
                            if own:
                                r0 = (s - 14) * 512 + tb * 128
                                P.dma("sp", lf_o[r0:r0 + 128, :], LOGF[:, blk, :], reads=[BLOGF], writes=[OUTB["lfo"]],
                                      sembuf=OUTB["lfo"])
                        P.dma("sp", vr_d[:, :, s * 4:(s + 1) * 4, :].rearrange("h p b e -> p h b e"), vst[:],
                              reads=[Bvst], writes=[SCR["vr"]], sembuf=Bvst)
                P.barrier()

                CK(20)
                with ExitStack() as ph2:
                    KS2 = [P.sbuf(f"KS2{i}", [128, 34], BF16, ph2) for i in range(2)]
                    KF2 = [P.sbuf(f"KF2{i}", [128, 34], F32, ph2) for i in range(2)]
                    VS2, BVS2 = P.sbuf("VS2", [128, 16, 65], BF16, ph2)
                    VS3, BVS3 = P.sbuf("VS3", [128, 16, 65], BF16, ph2)
                    VF2, BVF2 = P.sbuf("VF2", [32, 1024], F32, ph2)
                    TZ2, BTZ2 = P.sbuf("TZ2", [128, 16], F32, ph2)
                    P.op("pool", lambda e: e.memset(VS2[:], 1.0), writes=[BVS2])
                    P.op("pool", lambda e: e.memset(VS3[:], 1.0), writes=[BVS3])
                    for pr in range(8):
                        ps, Bps = nextps()
                        mm(P, ps[:, :34], [(wk(kc, pr), xT[:, kc, HALO0:HALO0 + 34])
                                           for kc in range(16)], [WS[pr // 4][1], BxT], [Bps])
                        kst, Bkst = KS2[pr % 2]
                        kf, Bkf = KF2[pr % 2]
                        evac(kst[:], ps[:, :34], [Bps], [Bkst])
                        evac(kf[:], ps[:, :34], [Bps], [Bkf])
                        P.dma("sp", ktr_d[pr * 128:(pr + 1) * 128, KC_HALO:KC_HALO + 34], kst[:], reads=[Bkst],
                              writes=[SCR["ktr"]], sembuf=Bkst)
                        P.dma("sp", kT_o[pr * 128:(pr + 1) * 128, HALO0:HALO0 + 34], kf[:], reads=[Bkf],
                              writes=[OUTB["kTo"]], sembuf=Bkf)
                    for (c0, nrow, vs, Bvs, blk) in [(HALO0, 2, VS2, BVS2, B_HALO), (SMP0, 32, VS3, BVS3, B_SMP)]:
                        for cg in range(2):
                            ps, Bps = nextps()
                            mm(P, ps[0:nrow, :512], [(xT[:, kc, c0:c0 + nrow], wv(kc, cg))
                                                     for kc in range(16)], [WS[2 + cg][1], BxT], [Bps])
                            evac(vs[0:nrow, cg * 8:(cg + 1) * 8, 0:64],
                                 ps[0:nrow, :512].rearrange("p (h d) -> p h d", d=64), [Bps], [Bvs])
                            evac(VF2[0:nrow, cg * 512:(cg + 1) * 512], ps[0:nrow, :512], [Bps], [BVF2])
                        P.dma("sp", vr_d[:, :, blk, :].rearrange("h p e -> p h e"), vs[:, :, :], reads=[Bvs],
                              writes=[SCR["vr"]], sembuf=Bvs)
                        P.dma("sp", v_o[c0:c0 + nrow, :], VF2[0:nrow, :], reads=[BVF2], writes=[OUTB["vo"]], sembuf=BVF2)
                        ps, Bps = nextps()
                        mm(P, ps[0:nrow, :16], [(xT[:, kc, c0:c0 + nrow], Wz[:, kc, :]) for kc in range(16)],
                           [BWz, BxT], [Bps])
                        logf_chain(ps[0:nrow, :16], nrow, LOGF[0:nrow, blk, :], Bps, TZ2, BTZ2)
                        P.dma("sp", lf_o[c0:c0 + nrow, :], LOGF[0:nrow, blk, :], reads=[BLOGF], writes=[OUTB["lfo"]],
                              sembuf=OUTB["lfo"])

                    CK(21)
                    WQ1, BWQ1 = P.sbuf("WQ1", [128, 16, 512], BF16, ph2)
                    P.dma("pool", WQ1[:], kp(w_in_d[:, 2048:2560]), writes=[BWQ1])
                    for g in range(2):
                        P.dma("pool", WS[g][0][:], kp(w_in_d[:, g * 512:(g + 1) * 512]), writes=[WS[g][1]])
                        P.dma("pool", WS[2 + g][0][:], kp(w_in_d[:, 1024 + g * 512:1024 + (g + 1) * 512]),
                              writes=[WS[2 + g][1]])
                    WQ = [(WQ1, BWQ1)] * 2
                    WQA = [P.sbuf(f"WQA{i}", [128, 16, 128], BF16, ph2) for i in range(2)]
                    QSTG = [P.sbuf(f"QSTG{i}", [128, NQ], BF16, ph2) for i in range(2)]
                    FT = [P.sbuf(f"FT{i}", [128, NQ], F32, ph2) for i in range(2)]
                    NF = [P.sbuf(f"NF{i}", [128, NQ], F32, ph2) for i in range(2)]
                    ZERO, BZERO = P.sbuf("ZERO", [128, 1024], BF16, ph2)
                    P.op("pool", lambda e: e.memset(ZERO[:], 0.0), writes=[BZERO])
                    for i in range(2):
                        P.op("pool", lambda e, i=i: e.memset(WQA[i][0][:], 0.0), writes=[WQA[i][1]])
                    NTILES = [(0, 512), (512, 512), (1024, 34)]
                    for h in range(16):
                        if h == 8:
                            P.dma("pool", WQ1[:], kp(w_in_d[:, 2048 + (h // 8) * 512:2048 + (h // 8 + 1) * 512]),
                                  writes=[BWQ1])
                        par = h % 2
                        wqa, Bwqa = WQA[par]
                        wq, Bwq = WQ[h // 8]
                        off = (h % 8) * 64
                        dlo = 0 if par == 0 else 64
                        flo = 64 if par == 0 else 0
                        P.op("pool", lambda e: e.tensor_copy(out=wqa[:, :, dlo:dlo + 64], in_=wq[:, :, off:off + 64]),
                             reads=[Bwq], writes=[Bwqa])
                        P.op("pool", lambda e: e.tensor_copy(out=wqa[:, :, flo:flo + 1], in_=Wz[:, :, h:h + 1]),
                             reads=[BWz], writes=[Bwqa])
                        P.op("pool", lambda e: e.tensor_copy(out=wqa[:, :, flo + 32:flo + 33], in_=Wz[:, :, h:h + 1]),
                             reads=[BWz], writes=[Bwqa])
                        qs, Bqs = QSTG[par]
                        ft, Bft = FT[par]
                        nf, Bnf = NF[par]
                        R = slice(flo, flo + 64)
                        for (c0, n) in NTILES:
                            ps, Bps = nextps()
                            mm(P, ps[:, :n], [(wqa[:, kc, :], xT[:, kc, c0:c0 + n]) for kc in range(16)],
                               [Bwqa, BxT], [Bps])
                            evac(qs[dlo:dlo + 64, c0:c0 + n], ps[dlo:dlo + 64, :n], [Bps], [Bqs], scale=0.125)
                            P.op("act", lambda e, ps=ps, c0=c0, n=n: e.activation(
                                out=ft[R, c0:c0 + n], in_=ps[R, :n], func=AF.Exp, scale=-1.0, bias=nbfb[R, h:h + 1]),
                                reads=[Bps, Bnbfb], writes=[Bft])
                        P.op("act", lambda e: e.activation(out=ft[R, :], in_=ft[R, :], func=AF.Ln, bias=1.0, scale=1.0),
                             reads=[Bft], writes=[Bft])
                        for (c0, n) in [(HALO0, 2), (SMP0, 32), (0, 1024)]:
                            P.op("dve", lambda e, c0=c0, n=n: e.tensor_tensor_scan(
                                out=nf[R, c0:c0 + n], data0=ft[R, c0:c0 + n], data1=ZERO[R, 0:n], initial=0.0,
                                op0=ALU.add, op1=ALU.add), reads=[Bft, BZERO], writes=[Bnf])
                        P.op("dve", lambda e: e.tensor_scalar(out=nf[R, 0:1024], in0=nf[R, 0:1024],
                                                              scalar1=nf[R, HALO0 + 1:HALO0 + 2], scalar2=None,
                                                              op0=ALU.add), reads=[Bnf], writes=[Bnf])
                        P.op("dve", lambda e: e.tensor_scalar(out=qs[R, :], in0=nf[R, :], scalar1=-1.0, scalar2=None,
                                                              op0=ALU.mult), reads=[Bnf], writes=[Bqs])
                        R2 = slice(flo + 32, flo + 64)
                        P.op("dve", lambda e: e.scalar_tensor_tensor(out=qs[R2, :], in0=nf[R2, :], scalar=-1.0,
                                                                     in1=qs[R2, :], op0=ALU.mult, op1=ALU.subtract),
                             reads=[Bnf, Bqs], writes=[Bqs])
                        P.dma("sp", qtr_d[h], qs[:], reads=[Bqs], writes=[SCR["qtr"]], sembuf=Bqs)
                    P.barrier()
                CK(30)
                with ExitStack() as ph3:
                    CIN, _ = P.sbuf("CIN", [128, 8, 1118], F32, ph3)
                    BCIN = [P.buf(f"cin{m}") for m in range(8)]
                    wc, Bwc = P.sbuf("wc", [128, 8, 31], F32, ph3)
                    cv3, Bcv3 = P.sbuf("cv3", [128, 8, 3], F32, ph3)
                    P.dma("sp", wc[:], wconvT_d, writes=[Bwc])
                    P.dma("sp", cv3[:], cvec_d, writes=[Bcv3])
                    for m in range(8):
                        P.dma("sp", CIN[:, m, 1056:1086], ccT_d[m * 128:(m + 1) * 128, :], writes=[BCIN[m]])
                    with ExitStack() as ph3a:
                        CK(40)
                        P.dma("sp", LOGF[:, B_CACHE:B_CACHE + 16, :], clf_d.rearrange("(b p) h -> p b h", p=128), writes=[BLOGF])
                        P.deferred = []
                        psr[:] = [5, 7]
                        with ExitStack() as ph4x:
                            ph4 = ph3a
                            CB, BCB = P.sbuf("CB", [128, NBLK, 16], F32, ph4)
                            BT, BBT = P.sbuf("BT", [128, NBLK, 16], F32, ph4)
                            RP, BRP = P.sbuf("RP", [128, 11, 16], F32, ph4)
                            NEGV, BNEGV = P.sbuf("NEGV", [128, 8], F32, ph4)
                            P.op("dve", lambda e: e.memset(CB[:], 0.0), writes=[BCB])
                            P.op("dve", lambda e: e.memset(RP[:], 0.0), writes=[BRP])
                            P.op("dve", lambda e: e.tensor_scalar(out=NEGV[:], in0=valid[:], scalar1=-NEG, scalar2=NEG,
                                                                  op0=ALU.mult, op1=ALU.add), reads=[Bvalid], writes=[BNEGV])
                            for (b0, nb, nrow) in [(0, 28, 128), (28, 28, 128), (56, 8, 128), (B_HALO, 1, 2), (B_SMP, 1, 32),
                                                   (B_CACHE, 16, 128)]:
                                ps1, Bps1 = nextps()
                                rhs = LOGF[0:nrow, b0:b0 + nb, :].rearrange("p b h -> p (b h)")
                                mm(P, ps1[:, :nb * 16], [(tri32[0:nrow, :], rhs)], [Btri32, BLOGF], [Bps1])
                                P.op("act", lambda e: e.copy(out=CB[0:nrow, b0:b0 + nb, :],
                                                             in_=ps1[0:nrow, :nb * 16].rearrange("p (b h) -> p b h", h=16)),
                                     reads=[Bps1], writes=[BCB])
                                ps2, Bps2 = nextps()
                                mm(P, ps2[:, :nb * 16], [(ones32[0:nrow, :], rhs)], [Bones32, BLOGF], [Bps2])
                                P.op("act", lambda e: e.copy(out=BT[:, b0:b0 + nb, :],
                                                             in_=ps2[:, :nb * 16].rearrange("p (b h) -> p b h", h=16)),
                                     reads=[Bps2], writes=[BBT])
                            chunks = [(8 * j, j) for j in range(7)] + [(B_OWN, 7), (B_CACHE, 8), (B_CACHE + 8, 9)]
                            for (b0, ci) in chunks:
                                for i in range(8):
                                    if i > 0:
                                        P.op("dve", lambda e, i=i: e.tensor_tensor(out=CB[:, b0 + i, :], in0=CB[:, b0 + i, :],
                                                                                   in1=RP[:, ci, :], op=ALU.add),
                                             reads=[BCB, BRP], writes=[BCB])
                                    P.op("dve", lambda e, i=i: e.tensor_tensor(out=RP[:, ci, :], in0=RP[:, ci, :],
                                                                               in1=BT[:, b0 + i, :], op=ALU.add),
                                         reads=[BRP, BBT], writes=[BRP])
                            RN, BRN = P.sbuf("RN", [128, 8, 16], F32, ph4)
                            P.op("dve", lambda e: e.memset(RN[:], 0.0), writes=[BRN])
                            for j in range(6, -1, -1):
                                P.op("dve", lambda e: e.tensor_tensor(out=RN[:, j, :], in0=RP[:, j, :], in1=RN[:, j + 1, :],
                                                                      op=ALU.add), reads=[BRP, BRN], writes=[BRN])
                                P.op("dve", lambda e: e.tensor_scalar(out=RN[:, j, :], in0=RN[:, j, :], scalar1=valid[:, j:j + 1],
                                                                      scalar2=None, op0=ALU.mult), reads=[BRN, Bvalid], writes=[BRN])
                            for j in range(7):
                                P.op("dve", lambda e: e.tensor_scalar(out=RN[:, j, :], in0=RN[:, j, :], scalar1=NEGV[:, j:j + 1],
                                                                      scalar2=None, op0=ALU.add), reads=[BRN, BNEGV], writes=[BRN])
                                for i in range(8):
                                    b = 8 * j + i
                                    P.op("dve", lambda e, b=b: e.scalar_tensor_tensor(out=DALL[:, b, :], in0=CB[:, b, :], scalar=-1.0,
                                                                                      in1=RN[:, j, :], op0=ALU.mult, op1=ALU.add),
                                         reads=[BCB, BRN], writes=[BDALL])
                            P.op("dve", lambda e: e.tensor_scalar(out=DALL[:, 0, :], in0=DALL[:, 0, :], scalar1=dneg[:, 0:1],
                                                                  scalar2=None, op0=ALU.add), reads=[BDALL, Bdneg], writes=[BDALL])
                            for i in range(8):
                                b = B_OWN + i
                                P.op("dve", lambda e, b=b: e.scalar_tensor_tensor(out=DALL[:, b, :], in0=CB[:, b, :], scalar=-1.0,
                                                                                  in1=BT[:, B_HALO, :], op0=ALU.mult,
                                                                                  op1=ALU.subtract),
                                     reads=[BCB, BBT], writes=[BDALL])
                            for (b, nrow) in [(B_HALO, 2), (B_SMP, 32)]:
                                P.op("dve", lambda e, b=b, nrow=nrow: e.tensor_scalar(out=DALL[0:nrow, b, :], in0=CB[0:nrow, b, :],
                                                                                      scalar1=-1.0, scalar2=None, op0=ALU.mult),
                                     reads=[BCB], writes=[BDALL])
                            HNEG, BHNEG = P.sbuf("HNEG", [128, 1], F32, ph4)
                            P.op("dve", lambda e: e.tensor_scalar(out=HNEG[:], in0=hmask[:], scalar1=-NEG, scalar2=NEG,
                                                                  op0=ALU.mult, op1=ALU.add), reads=[Bhmask], writes=[BHNEG])
                            P.op("dve", lambda e: e.tensor_scalar(out=DALL[0:2, B_HALO, :], in0=DALL[0:2, B_HALO, :],
                                                                  scalar1=HNEG[0:2, 0:1], scalar2=None, op0=ALU.add),
                                 reads=[BDALL, BHNEG], writes=[BDALL])
                            P.op("dve", lambda e: e.tensor_tensor(out=RP[:, 8, :], in0=RP[:, 8, :], in1=RP[:, 9, :], op=ALU.add),
                                 reads=[BRP], writes=[BRP])
                            for ci in range(2):
                                for i in range(8):
                                    b = B_CACHE + 8 * ci + i
                                    P.op("dve", lambda e, b=b, ci=ci: e.scalar_tensor_tensor(
                                        out=DALL[:, b, :], in0=CB[:, b, :], scalar=-1.0, in1=RP[:, 8 + ci, :], op0=ALU.mult,
                                        op1=ALU.add), reads=[BCB, BRP], writes=[BDALL])
                            pass

                        P.defq = P.deferred
                        P.deferred = None
                        psr[:] = [0, 5]

                        SG = [P.sbuf(f"SG{i}", [128, 512], F32, ph3a) for i in range(2)]
                        SEGS = {0: [(0, 512, 32)], 512: [(0, 512, 544)], 1024: [(0, 2, 30), (2, 32, 1086), (34, 30, 0)]}
                        cnt = 0
                        for g in range(2):
                            ta, Bta = WS[g]
                            tg, Btg = WS[2 + g]
                            for m4 in range(4):
                                m = g * 4 + m4
                                for (c0, n) in [(0, 512), (512, 512), (1024, 64)]:
                                    psA, BpsA = nextps()
                                    mm(P, psA[:, :n], [(ta[:, kc, m4 * 128:(m4 + 1) * 128], xT[:, kc, c0:c0 + n])
                                                       for kc in range(16)], [Bta, BxT], [BpsA])
                                    psG, BpsG = nextps()
                                    mm(P, psG[:, :n], [(tg[:, kc, m4 * 128:(m4 + 1) * 128], xT[:, kc, c0:c0 + n])
                                                       for kc in range(16)], [Btg, BxT], [BpsG])
                                    sg, Bsg = SG[cnt % 2]
                                    cnt += 1
                                    P.flush(12)
                                    P.op("act", lambda e: e.activation(out=sg[:, :n], in_=psG[:, :n], func=AF.Sigmoid),
                                         reads=[BpsG], writes=[Bsg])
                                    for (s0, ln, d0) in SEGS[c0]:
                                        P.op("dve", lambda e, s0=s0, ln=ln, d0=d0: e.tensor_tensor(
                                            out=CIN[:, m, d0:d0 + ln], in0=psA[:, s0:s0 + ln], in1=sg[:, s0:s0 + ln],
                                            op=ALU.mult), reads=[BpsA, Bsg], writes=[BCIN[m]])
                        P.barrier()
                        psr[:] = [0, 7]
                    P.dma("sp", convT_o[:, 0:30].rearrange("(m p) t -> p m t", p=128), CIN[:, :, 1026:1056],
                          reads=BCIN, writes=[OUTB["convTo"]], sembuf=OUTB["convTo"])
                    P.dma("sp", convT_o[:, 30:60].rearrange("(m p) t -> p m t", p=128), CIN[:, :, 1088:1118],
                          reads=BCIN, writes=[OUTB["convTo"]], sembuf=OUTB["convTo"])
                    CK(31)
                    with ExitStack() as ph3t:
                        ACC4 = [P.sbuf(f"ACCD{i}", [128, 1088], F32, ph3t) for i in range(4)]

                        def tap_init(m):
                            a0, Ba0 = ACC4[(m % 2) * 2]
                            a1, Ba1 = ACC4[(m % 2) * 2 + 1]
                            P.op("act", lambda e: e.activation(out=a0[:], in_=CIN[:, m, 0:1088], func=AF.Identity,
                                                               scale=wc[:, m, 0:1], bias=cv3[:, m, 0:1]),
                                 reads=[BCIN[m], Bwc, Bcv3], writes=[Ba0])
                            P.op("act", lambda e: e.activation(out=a1[:], in_=CIN[:, m, 1:1089], func=AF.Identity,
                                                               scale=wc[:, m, 1:2]),
                                 reads=[BCIN[m], Bwc], writes=[Ba1])
                        tap_init(0)
                        for m in range(8):
                            ACC2 = ACC4[(m % 2) * 2:(m % 2) * 2 + 2]
                            a0, Ba0 = ACC2[0]
                            a1, Ba1 = ACC2[1]
                            for j in range(2, 31):
                                ac, Bac = ACC2[j % 2]
                                P.op("dve", lambda e, j=j, ac=ac: e.scalar_tensor_tensor(
                                    out=ac[:], in0=CIN[:, m, j:j + 1088], scalar=wc[:, m, j:j + 1], in1=ac[:],
                                    op0=ALU.mult, op1=ALU.add), reads=[BCIN[m], Bac], writes=[Bac])
                            if m + 1 < 8:
                                tap_init(m + 1)
                            P.op("dve", lambda e: e.tensor_tensor(out=a0[:], in0=a0[:], in1=a1[:], op=ALU.add),
                                 reads=[Ba0, Ba1], writes=[Ba0])
                            P.op("act", lambda e: e.copy(out=CIN[:, m, 30:1118], in_=a0[:]), reads=[Ba0],
                                 writes=[BCIN[m]])
                        P.barrier()
                    with ExitStack() as ph3b:
                        MEAN, BMEAN = P.sbuf("MEAN", [128, 1088], F32, ph3b)
                        RSTD, BRSTD = P.sbuf("RSTD", [128, 1088], F32, ph3b)
                        SQ = [P.sbuf(f"SQ{i}", [128, 512], F32, ph3b) for i in range(2)]
                        TMPV, BTMPV = P.sbuf("TMPV", [128, 512], F32, ph3b)
                        CK(32)
                        for (o0, n) in [(0, 512), (512, 512), (1024, 64)]:
                            psS, BpsS = nextps()
                            mm(P, psS[:, :n], [(ones32[:, :], CIN[:, m, 30 + o0:30 + o0 + n]) for m in range(8)],
                               BCIN + [Bones32], [BpsS])
                            psQ, BpsQ = nextps()
                            for m in range(8):
                                sq, Bsq = SQ[m % 2]
                                P.op("act", lambda e: e.activation(out=sq[:, :n], in_=CIN[:, m, 30 + o0:30 + o0 + n],
                                                                   func=AF.Square), reads=[BCIN[m]], writes=[Bsq])
                                P.op("pe", lambda e, m=m: e.matmul(psQ[:, :n], lhsT=ones32[:, :], rhs=sq[:, :n],
                                                                   start=(m == 0), stop=(m == 7)),
                                     reads=[Bsq, Bones32], writes=[BpsQ])
                            P.op("act", lambda e: e.mul(MEAN[:, o0:o0 + n], psS[:, :n], 1.0 / 1024), reads=[BpsS],
                                 writes=[BMEAN])
                            P.op("dve", lambda e: e.tensor_tensor(out=TMPV[:, :n], in0=MEAN[:, o0:o0 + n],
                                                                  in1=MEAN[:, o0:o0 + n], op=ALU.mult),
                                 reads=[BMEAN], writes=[BTMPV])
                            P.op("dve", lambda e: e.scalar_tensor_tensor(out=RSTD[:, o0:o0 + n], in0=psQ[:, :n],
                                                                         scalar=1.0 / 1024, in1=TMPV[:, :n],
                                                                         op0=ALU.mult, op1=ALU.subtract),
                                 reads=[BpsQ, BTMPV], writes=[BRSTD])
                            P.op("act", lambda e: e.activation(out=RSTD[:, o0:o0 + n], in_=RSTD[:, o0:o0 + n], func=AF.Ln,
                                                               bias=LN_EPS, scale=1.0), reads=[BRSTD], writes=[BRSTD])
                            P.op("act", lambda e: e.activation(out=RSTD[:, o0:o0 + n], in_=RSTD[:, o0:o0 + n], func=AF.Exp,
                                                               scale=-0.5), reads=[BRSTD], writes=[BRSTD])
                        for m in range(8):
                            cm = CIN[:, m, 30:1118]
                            P.op("dve", lambda e: e.tensor_tensor(out=cm, in0=cm, in1=MEAN[:], op=ALU.subtract),
                                 reads=[BCIN[m], BMEAN], writes=[BCIN[m]])
                            P.op("dve", lambda e: e.tensor_tensor(out=cm, in0=cm, in1=RSTD[:], op=ALU.mult),
                                 reads=[BCIN[m], BRSTD], writes=[BCIN[m]])
                            for (s0, ln, d0) in [(2, 1024, 0), (0, 2, HALO0), (1056, 32, SMP0)]:
                                P.op("act", lambda e, s0=s0, ln=ln, d0=d0: e.activation(
                                    out=MIXT[:, m, d0:d0 + ln], in_=CIN[:, m, 30 + s0:30 + s0 + ln], func=AF.Silu,
                                    scale=cv3[:, m, 1:2], bias=cv3[:, m, 2:3]), reads=[BCIN[m], Bcv3], writes=[BMIX[m]])
                        P.barrier()
                P.barrier()

            P.barrier()

            WSTREAM = [(wd, g) for wd in (w_out_d, w_mk_d, w_mv_d, w_mq_d, w_mo_d) for g in range(4)]
            wsi = [0]

            def ws_issue():
                i = wsi[0]
                if i < len(WSTREAM):
                    wd, g = WSTREAM[i]
                    P.dma("pool", WS[i % 4][0][:], kp(wd[:, g * 512:(g + 1) * 512]), writes=[WS[i % 4][1]])
                    wsi[0] += 1
            for _ in range(4):
                ws_issue()

            CK(50)
            with ExitStack() as ph5:
                KALL = [P.sbuf(f"KALL{i}", [128, NKEY], BF16, ph5) for i in range(2)]
                VH = [P.sbuf(f"VH{i}", [128, NBLK, 65], BF16, ph5) for i in range(2)]
                QH = [P.sbuf(f"QH{i}", [128, NQ], BF16, ph5) for i in range(2)]
                PT = [P.sbuf(f"PT{i}", [128, 512], BF16, ph5) for i in range(8)]
                OS = [P.sbuf(f"OS{i}", [65, 512], F32, ph5) for i in range(4)]
                ONB = [P.sbuf(f"ONB{i}", [64, 512], BF16, ph5) for i in range(2)]
                shiftb, Bshiftb = P.sbuf("shiftb", [64, 128], BF16, ph5)
                P.dma("pool", shiftb[:], shift_d, writes=[Bshiftb])
                P.op("dve", lambda e: e.memset(VH[0][0][:], 1.0), writes=[VH[0][1]])
                P.op("dve", lambda e: e.memset(VH[1][0][:], 1.0), writes=[VH[1][1]])
                for par in range(2):
                    k_, Bk_ = KALL[par]
                    P.op("dve", lambda e: e.memset(k_[:], 0.0), writes=[Bk_])
                    f0 = 64 if par == 0 else 0
                    P.op("dve", lambda e: e.memset(k_[f0:f0 + 1, :], 1.0), writes=[Bk_])
                    P.op("dve", lambda e: e.memset(k_[f0 + 32:f0 + 33, :], 1.0), writes=[Bk_])
                PSS = [PS[i] for i in range(0, 4)]
                PSO = [PS[4], PS[5], PS[6], PS[7]]
                BKC = [P.buf("kallc0"), P.buf("kallc1")]
                BVC = [P.buf("vhc0"), P.buf("vhc1")]
                scnt = [0]
                ocnt = [0]
                pcnt = [0]
                pending = []

                def att_jobs(h, par, jobs):
                    kal, Bkal = KALL[par]
                    vh, Bvh = VH[h % 2]
                    qh, Bqh = QH[h % 2]
                    kreads = [[Bkal] + ([BKC[par]] if any(b[0] >= B_CACHE for b in j[2]) else []) for j in jobs]
                    vreads = [[Bvh] + ([BVC[h % 2]] if any(b[0] >= B_CACHE for b in j[2]) else []) for j in jobs]
                    LA = 3
                    pi = pcnt[0]
                    pcnt[0] += 1
                    st = []
                    for ji, (q0, nq, blocks, mixcols) in enumerate(jobs):
                        st.append({"pso": PSO[(pi % 2) * 2 + ji], "slots": []})

                    def stage_a(ji, bi):
                        q0, nq, blocks, mixcols = jobs[ji]
                        blk, nk, kcol, qlo, mask = blocks[bi]
                        pss, Bpss = PSS[scnt[0] % 4]
                        scnt[0] += 1
                        pt, Bpt = PT[ocnt[0] % len(PT)]
                        ocnt[0] += 1
                        st[ji]["slots"].append((pt, Bpt))
                        w = nq - qlo
                        P.op("pe", lambda e: e.matmul(pss[0:nk, 0:w], lhsT=kal[:, kcol:kcol + nk],
                                                      rhs=qh[:, q0 + qlo:q0 + nq], start=True, stop=True),
                             reads=kreads[ji] + [Bqh], writes=[Bpss])
                        P.op("act", lambda e: e.activation(out=pt[0:nk, 0:w], in_=pss[0:nk, 0:w], func=AF.Exp,
                                                           bias=DALL[0:nk, blk, h:h + 1], scale=1.0),
                             reads=[Bpss, BDALL], writes=[Bpt])
                        if mask:
                            P.op("dve", lambda e: e.tensor_tensor(out=pt[0:nk, 0:nk], in0=pt[0:nk, 0:nk],
                                                                   in1=trib[0:nk, 0:nk], op=ALU.mult),
                                 reads=[Bpt, Btrib], writes=[Bpt])

                    def stage_b(ji, bi):
                        q0, nq, blocks, mixcols = jobs[ji]
                        blk, nk, kcol, qlo, mask = blocks[bi]
                        pt, Bpt = st[ji]["slots"][bi]
                        pso, Bpso = st[ji]["pso"]
                        w = nq - qlo
                        nb = len(blocks)
                        P.op("pe", lambda e: e.matmul(pso[0:65, qlo:nq], lhsT=vh[0:nk, blk, :], rhs=pt[0:nk, 0:w],
                                                      start=(bi == 0), stop=(bi == nb - 1)),
                             reads=vreads[ji] + [Bpt], writes=[Bpso])
                    maxnb = max(len(j[2]) for j in jobs)
                    for i in range(maxnb + LA):
                        for ji in range(len(jobs)):
                            if i < len(jobs[ji][2]):
                                stage_a(ji, i)
                        for ji in range(len(jobs)):
                            if 0 <= i - LA < len(jobs[ji][2]):
                                stage_b(ji, i - LA)
                        if i >= 2 and pending:
                            pending.pop(0)()
                    while pending:
                        pending.pop(0)()
                    for ji, (q0, nq, blocks, mixcols) in enumerate(jobs):
                        pso, Bpso = st[ji]["pso"]
                        osb, Bosb = OS[(pi % 2) * 2 + ji]
                        onb, Bonb = ONB[pi % 2]
                        ch = 8 + h // 2

                        def t1(pso=pso, Bpso=Bpso, osb=osb, Bosb=Bosb, nq=nq):
                            P.op("dve", lambda e: e.tensor_copy(out=osb[:, :nq], in_=pso[0:65, :nq]), reads=[Bpso],
                                 writes=[Bosb])

                        def t2(osb=osb, Bosb=Bosb, nq=nq):
                            P.op("dve", lambda e: e.tensor_scalar(out=osb[64:65, :nq], in0=osb[64:65, :nq],
                                                                  scalar1=1e-30, scalar2=None, op0=ALU.add),
                                 reads=[Bosb], writes=[Bosb])

                        def t3(osb=osb, Bosb=Bosb, nq=nq):
                            P.op("dve", lambda e: e.reciprocal(out=osb[64:65, :nq], in_=osb[64:65, :nq]), reads=[Bosb],
                                 writes=[Bosb])

                        def t4(pso=pso, Bpso=Bpso, osb=osb, Bosb=Bosb, nq=nq):
                            P.op("pe", lambda e: e.matmul(pso[:, :nq], lhsT=ones32[64:65, :], rhs=osb[64:65, :nq],
                                                          start=True, stop=True), reads=[Bosb, Bones32], writes=[Bpso])
                        pending.extend([t1, t2, t3, t4])
                        if par == 0:
                            def t5(pso=pso, Bpso=Bpso, osb=osb, Bosb=Bosb, nq=nq, mixcols=mixcols, ch=ch):
                                P.op("dve", lambda e: e.tensor_tensor(out=MIXT[0:64, ch, mixcols:mixcols + nq],
                                                                      in0=osb[0:64, :nq], in1=pso[0:64, :nq],
                                                                      op=ALU.mult),
                                     reads=[Bosb, Bpso], writes=[BMIX[ch]])
                            pending.append(t5)
                        else:
                            def t5(pso=pso, Bpso=Bpso, osb=osb, Bosb=Bosb, nq=nq, onb=onb, Bonb=Bonb):
                                P.op("dve", lambda e: e.tensor_tensor(out=onb[:, :nq], in0=osb[0:64, :nq],
                                                                      in1=pso[0:64, :nq], op=ALU.mult),
                                     reads=[Bosb, Bpso], writes=[Bonb])

                            def t6(pso=pso, Bpso=Bpso, nq=nq, onb=onb, Bonb=Bonb):
                                P.op("pe", lambda e: e.matmul(pso[:, :nq], lhsT=shiftb[:, :], rhs=onb[:, :nq],
                                                              start=True, stop=True), reads=[Bonb, Bshiftb],
                                     writes=[Bpso])

                            def t7(pso=pso, Bpso=Bpso, nq=nq, mixcols=mixcols, ch=ch):
                                P.op("dve", lambda e: e.tensor_copy(out=MIXT[64:128, ch, mixcols:mixcols + nq],
                                                                    in_=pso[64:128, :nq]), reads=[Bpso],
                                     writes=[BMIX[ch]])
                            pending.extend([t5, t6, t7])

                REM = [(b, 128, 128 * b, 0, False) for b in range(56)]
                HAL = (B_HALO, 2, KC_HALO, 0, False)
                for h in range(16):
                    CK(51 + h)
                    par = h % 2
                    dlo = 0 if par == 0 else 64
                    kal, Bkal = KALL[par]
                    vh, Bvh = VH[h % 2]
                    qh, Bqh = QH[h % 2]
                    P.dma("sp", kal[dlo:dlo + 64, 0:KC_CACHE], ktr_d[h * 64:(h + 1) * 64, :], reads=[SCR["ktr"]],
                          writes=[Bkal])
                    P.dma("pool", kal[dlo:dlo + 64, KC_CACHE:NKEY], ckT_d[h * 64:(h + 1) * 64, :], reads=[Bkal], writes=[BKC[par]])
                    P.dma("sp", vh[:, 0:B_CACHE, :], vr_d[h], reads=[SCR["vr"]], writes=[Bvh])
                    P.dma("pool", vh[:, B_CACHE:NBLK, 0:64],
                          cv_d[:, h * 64:(h + 1) * 64].rearrange("(b p) d -> p b d", p=128), reads=[Bvh],
                          writes=[BVC[h % 2]])
                    P.dma("sp", qh[:], qtr_d[h], reads=[SCR["qtr"]], writes=[Bqh])
                    blocks0 = REM + [HAL] + [(B_OWN + kb, 128, KC_OWN + 128 * kb, 128 * kb, True) for kb in range(4)]
                    blocks1 = REM + [HAL] + [(B_OWN + kb, 128, KC_OWN + 128 * kb, 0, False) for kb in range(4)] + \
                        [(B_OWN + kb, 128, KC_OWN + 128 * kb, 128 * (kb - 4), True) for kb in range(4, 8)]
                    blocks_h = REM + [(B_HALO, 2, KC_HALO, 0, True)]
                    blocks_s = [(B_CACHE + b, 128, KC_CACHE + 128 * b, 0, False) for b in range(16)] + \
                        [(B_SMP, 32, KC_SMP, 0, True)]
                    att_jobs(h, par, [(0, 512, blocks0, 0), (HALO0, 2, blocks_h, HALO0)])
                    att_jobs(h, par, [(512, 512, blocks1, 512), (SMP0, 32, blocks_s, SMP0)])
                while pending:
                    pending.pop(0)()
                P.barrier()
            P.barrier()
            def load_dd(wd):
                for g in range(4):
                    P.dma("pool", WS[g][0][:], kp(wd[:, g * 512:(g + 1) * 512]), writes=[WS[g][1]])

            TBLK = [(tb * 128, 128) for tb in range(8)] + [(1024, 34)]

            def proj_ln(srcT, Bsrc, res_d, lnidx, dstT, BdstT, out_d, out_buf, scope, prefix, wtiles=None,
                        zsrc_d=None):
                Z = [P.sbuf(f"{prefix}Z{i}", [128, D], F32, scope) for i in range(3)]
                ZB = [P.sbuf(f"{prefix}ZB{i}", [128, D], BF16, scope) for i in range(2)]
                LNP, BLNP = P.sbuf(f"{prefix}LNP", [128, 2, D], F32, scope)
                ST = [P.sbuf(f"{prefix}ST{i}", [128, 4, 6], F32, scope) for i in range(2)]
                MV = [P.sbuf(f"{prefix}MV{i}", [128, 4], F32, scope) for i in range(2)]
                P.dma("sp", LNP[:, 0, :], lnp_d[lnidx], writes=[BLNP])
                P.dma("sp", LNP[:, 1, :], lnp_d[lnidx + 1], writes=[BLNP])
                nblk = len(TBLK)

                def load(ti):
                    c0, nt = TBLK[ti]
                    z, Bz = Z[ti % 3]
                    P.dma("sp", z[0:nt, :], zsrc_d[c0:c0 + nt, :], reads=[BZSCR], writes=[Bz])

                def s1a(ti):
                    c0, nt = TBLK[ti]
                    z, Bz = Z[ti % 3]
                    st, Bst = ST[ti % 2]
                    mv, Bmv = MV[ti % 2]
                    for g in range(4):
                        P.op("dve", lambda e, g=g: e.bn_stats(out=st[0:nt, g, :], in_=z[0:nt, g * 512:(g + 1) * 512]),
                             reads=[Bz], writes=[Bst])
                    P.op("dve", lambda e: e.bn_aggr(out=mv[0:nt, 0:2], in_=st[0:nt, :, :].rearrange("p a b -> p (a b)")),
                         reads=[Bst], writes=[Bmv])

                def s1b(ti):
                    c0, nt = TBLK[ti]
                    mv, Bmv = MV[ti % 2]
                    P.op("act", lambda e: e.activation(out=mv[0:nt, 1:2], in_=mv[0:nt, 1:2], func=AF.Ln, bias=LN_EPS,
                                                       scale=1.0), reads=[Bmv], writes=[Bmv])
                    P.op("act", lambda e: e.activation(out=mv[0:nt, 1:2], in_=mv[0:nt, 1:2], func=AF.Exp, scale=-0.5),
                         reads=[Bmv], writes=[Bmv])

                def s1c(ti):
                    c0, nt = TBLK[ti]
                    mv, Bmv = MV[ti % 2]
                    P.op("dve", lambda e: e.scalar_tensor_tensor(out=mv[0:nt, 2:3], in0=mv[0:nt, 0:1], scalar=-1.0,
                                                                 in1=mv[0:nt, 1:2], op0=ALU.mult, op1=ALU.mult),
                         reads=[Bmv], writes=[Bmv])

                def s2a1(ti):
                    c0, nt = TBLK[ti]
                    z, Bz = Z[ti % 3]
                    mv, Bmv = MV[ti % 2]
                    P.op("act", lambda e: e.activation(out=z[0:nt, :], in_=z[0:nt, :], func=AF.Identity,
                                                       scale=mv[0:nt, 1:2], bias=mv[0:nt, 2:3]),
                         reads=[Bz, Bmv], writes=[Bz])

                def s2a2(ti):
                    c0, nt = TBLK[ti]
                    z, Bz = Z[ti % 3]
                    P.op("dve", lambda e: e.tensor_tensor(out=z[0:nt, :], in0=z[0:nt, :], in1=LNP[0:nt, 0, :],
                                                          op=ALU.mult), reads=[Bz, BLNP], writes=[Bz])
                    P.op("dve", lambda e: e.tensor_tensor(out=z[0:nt, :], in0=z[0:nt, :], in1=LNP[0:nt, 1, :],
                                                          op=ALU.add), reads=[Bz, BLNP], writes=[Bz])

                def s2a3(ti):
                    c0, nt = TBLK[ti]
                    z, Bz = Z[ti % 3]
                    P.dma("sp", out_d[c0:c0 + nt, :], z[0:nt, :], reads=[Bz], writes=[out_buf], sembuf=Bz)
                    if dstT is not None:
                        zb, Bzb = ZB[ti % 2]
                        P.op("act", lambda e: e.copy(out=zb[0:nt, :], in_=z[0:nt, :]), reads=[Bz], writes=[Bzb])

                def s2b(ti):
                    if dstT is None:
                        return
                    c0, nt = TBLK[ti]
                    zb, Bzb = ZB[ti % 2]
                    for half in range(2):
                        for j in range(8):
                            kc = half * 8 + j
                            P.op("pe", lambda e, kc=kc, j=j: e.transpose(
                                out=PST[:, j * 128:j * 128 + nt], in_=zb[0:nt, kc * 128:(kc + 1) * 128],
                                identity=identb[0:nt, 0:nt]), reads=[Bzb, Bident], writes=[BPST])
                        evac(dstT[:, half * 8:(half + 1) * 8, c0:c0 + nt],
                             PST[:, :].rearrange("p (j t) -> p j t", t=128)[:, :, 0:nt], [BPST], BdstT)
                load(0)
                load(1)
                s1a(0)
                s1b(0)
                s1c(0)
                for ti in range(nblk):
                    if ti + 2 < nblk:
                        load(ti + 2)
                    s2a1(ti)
                    if ti + 1 < nblk:
                        s1a(ti + 1)
                        s1b(ti + 1)
                    s2a2(ti)
                    if ti + 1 < nblk:
                        s1c(ti + 1)
                    s2a3(ti)
                    if ti >= 1:
                        s2b(ti - 1)
                s2b(nblk - 1)

            x2t_d = dscr("x2t", [128, 16, NT], BF16)
            BX2T = P.buf("x2t")
            zscr_d = dscr("zscr", [9 * 128, D], F32)
            BZSCR = P.buf("zscr")
            def stream_proj(srcT, Bsrc, res_d, res_bufs, t0, scope, prefix):
                XB3 = [P.sbuf(f"{prefix}XB3{i}", [128, 512], F32, scope) for i in range(3)]
                Z3 = [P.sbuf(f"{prefix}Z3{i}", [128, 512], F32, scope) for i in range(3)]
                steps = [(g, ti) for g in range(4) for ti in range(len(TBLK))]

                def ld(si):
                    g_, ti_ = steps[si]
                    c0_, nt_ = TBLK[ti_]
                    xb_, Bxb_ = XB3[si % 3]
                    P.dma("sp", xb_[0:nt_, :], res_d[c0_:c0_ + nt_, g_ * 512:(g_ + 1) * 512], reads=res_bufs,
                          writes=[Bxb_])
                ld(0)
                for g in range(4):
                    wt, Bwt = WS[(t0 + g) % 4]
                    for ti, (c0, nt) in enumerate(TBLK):
                        si = g * len(TBLK) + ti
                        if si + 1 < len(steps):
                            ld(si + 1)
                        xb, Bxb = XB3[si % 3]
                        z3, Bz3 = Z3[si % 3]
                        ps, Bps = nextps()
                        mm(P, ps[0:nt, :512], [(srcT[:, kc, c0:c0 + nt], wt[:, kc, :]) for kc in range(16)],
                           [Bwt] + Bsrc, [Bps])
                        P.op("dve", lambda e: e.scalar_tensor_tensor(out=z3[0:nt, :], in0=xb[0:nt, :], scalar=ALPHA,
                                                                     in1=ps[0:nt, :512], op0=ALU.mult, op1=ALU.add),
                             reads=[Bxb, Bps], writes=[Bz3])
                        P.dma("sp", zscr_d[c0:c0 + nt, g * 512:(g + 1) * 512], z3[0:nt, :], reads=[Bz3],
                              writes=[BZSCR], sembuf=Bz3)
                    ws_issue()

            with ExitStack() as ph34:
                XT, BXT = P.sbuf("XT", [128, 16, NT], BF16, ph34)
                P.op("pool", lambda e: e.memset(XT[:], 0.0), writes=[BXT])
                CK(70)
                with ExitStack() as ph6:
                    stream_proj(MIXT, BMIX, xtm_d, [], 0, ph6, "a")
                    proj_ln(None, None, None, 0, XT, [BXT], xres_d, SCR["xres"], ph6, "a", zsrc_d=zscr_d)
                    P.barrier()
                P.barrier()

                CK(80)
                with ExitStack() as ph7:
                    MKT, BMKT = P.sbuf("MKT", [128, 16, 512], BF16, ph7)
                    MVT, BMVT = P.sbuf("MVT", [128, 4, D], BF16, ph7)
                    with ExitStack() as ph7a:
                        memT, BmemT = P.sbuf("memT", [128, 16, 256], BF16, ph7a)
                        MF = [P.sbuf(f"MF{i}", [128, 512], F32, ph7a) for i in range(2)]
                        P.dma("pool", memT[:], kp(memT_d), writes=[BmemT])
                        P.dma("pool", MKT[:, :, 256:512], kp(cmkT_d), writes=[BMKT])
                        P.dma("pool", MVT[:, 2:4, :], cmv_d.rearrange("(b p) n -> p b n", p=128), writes=[BMVT])
                        for (t0, od, obuf, is_k) in [(4, memk_o, OUTB["memk"], True), (8, memv_o, OUTB["memv"], False)]:
                            for g in range(4):
                                wt, Bwt = WS[(t0 + g) % 4]
                                for blk in range(2):
                                    ps, Bps = nextps()
                                    mm(P, ps[:, :512], [(memT[:, kc, blk * 128:(blk + 1) * 128], wt[:, kc, :])
                                                        for kc in range(16)], [Bwt, BmemT], [Bps])
                                    mf, Bmf = MF[blk % 2]
                                    evac(mf[:], ps[:, :512], [Bps], [Bmf])
                                    P.dma("sp", od[blk * 128:(blk + 1) * 128, g * 512:(g + 1) * 512], mf[:], reads=[Bmf],
                                          writes=[obuf], sembuf=Bmf)
                                    if not is_k:
                                        evac(MVT[:, blk, g * 512:(g + 1) * 512], ps[:, :512], [Bps], [BMVT])
                                if is_k:
                                    for c in range(4):
                                        m = 4 * g + c
                                        ps, Bps = nextps()
                                        mm(P, ps[:, :256], [(wt[:, kc, c * 128:(c + 1) * 128], memT[:, kc, :])
                                                            for kc in range(16)], [Bwt, BmemT], [Bps])
                                        evac(MKT[:, m, 0:256], ps[:, :256], [Bps], [BMKT])
                                ws_issue()
                        P.barrier()
                    CK(81)
                    QMT = [P.sbuf(f"QMT{i}", [128, 4, NQ], BF16, ph7) for i in range(2)]
                    PT2 = [P.sbuf(f"PT2{i}", [128, 2, 512], BF16, ph7) for i in range(2)]
                    RD = [P.sbuf(f"RD{i}", [128, 512], F32, ph7) for i in range(2)]
                    tcnt = 0
                    for hm in range(4):
                        qmt, Bqmt = QMT[hm % 2]
                        wt, Bwt = WS[(12 + hm) % 4]
                        for c in range(4):
                            for (c0, n) in [(0, 512), (512, 512), (1024, 34)]:
                                ps, Bps = nextps()
                                mm(P, ps[:, :n], [(wt[:, kc, c * 128:(c + 1) * 128], XT[:, kc, c0:c0 + n])
                                                  for kc in range(16)], [Bwt, BXT], [Bps])
                                evac(qmt[:, c, c0:c0 + n], ps[:, :n], [Bps], [Bqmt], scale=512.0 ** -0.5)
                        ws_issue()
                        for (c0, n, mc0, vb0) in [(0, 512, 0, 0), (512, 512, 0, 0), (HALO0, 2, 0, 0), (SMP0, 32, 256, 2)]:
                            pt, Bpt = PT2[tcnt % 2]
                            rd, Brd = RD[tcnt % 2]
                            tcnt += 1
                            for mb in range(2):
                                ps, Bps = nextps()
                                mm(P, ps[:, :n], [(MKT[:, 4 * hm + c, mc0 + mb * 128:mc0 + (mb + 1) * 128],
                                                   qmt[:, c, c0:c0 + n]) for c in range(4)], [BMKT, Bqmt], [Bps])
                                P.op("act", lambda e, mb=mb: e.activation(out=pt[:, mb, :n], in_=ps[:, :n], func=AF.Exp),
                                     reads=[Bps], writes=[Bpt])
                            psd, Bpsd = nextps()
                            mm(P, psd[:, :n], [(onesb[:, :], pt[:, mb, :n]) for mb in range(2)], [Bonesb, Bpt], [Bpsd])
                            P.op("dve", lambda e: e.reciprocal(out=rd[:, :n], in_=psd[:, :n]), reads=[Bpsd], writes=[Brd])
                            for c in range(4):
                                ch = 4 * hm + c
                                ps, Bps = nextps()
                                mm(P, ps[:, :n], [(MVT[:, vb0 + mb, ch * 128:(ch + 1) * 128], pt[:, mb, :n])
                                                  for mb in range(2)], [BMVT, Bpt], [Bps])
                                P.op("dve", lambda e, ch=ch: e.tensor_tensor(out=MIXT[:, ch, c0:c0 + n], in0=ps[:, :n],
                                                                             in1=rd[:, :n], op=ALU.mult),
                                     reads=[Bps, Brd], writes=[BMIX[ch]])
                    P.barrier()
                P.barrier()
                CK(85)
                with ExitStack() as ph8:
                    stream_proj(MIXT, BMIX, xres_d, [SCR["xres"]], 16, ph8, "b")
                    for (slot, col0) in [(0, 0), (1, DFF), (2, 512)]:
                        P.dma("pool", WS[slot][0][:], kp(w_up_d[:, col0:col0 + 512]), writes=[WS[slot][1]])
                    proj_ln(None, None, None, 2, XT, [BXT], xres_d, SCR["xres"], ph8, "b", zsrc_d=zscr_d)
                    P.dma("sp", x2t_d, XT[:], reads=[BXT], writes=[BX2T], sembuf=BX2T)
                    P.barrier()
                P.barrier()
            P.barrier()
            phA.close()
            P.barrier()

            CK(90)
            with ExitStack() as ph9:
                ACTT, BACTT = P.sbuf("ACTT", [128, 44, NQ], BF16, ph9)
                wff, Bwff = P.sbuf("wff", [128, 88, 3], F32, ph9)
                bff, Bbff = P.sbuf("bff", [128, 88], F32, ph9)
                CF, BCF = P.sbuf("CF", [128, 88, 2], F32, ph9)
                HOUT, BHOUT = P.sbuf("HOUT", [128, 88, 4], F32, ph9)
                P.dma("sp", wff[:], wffnT_d, writes=[Bwff])
                P.dma("sp", bff[:], bffn_d, writes=[Bbff])
                P.dma("sp", CF[:], cffT_d.rearrange("(c p) t -> p c t", p=128), writes=[BCF])
                with ExitStack() as ph9a:
                    X2, BX2 = P.sbuf("X2", [128, 16, NT], BF16, ph9a)
                    P.dma("sp", X2[:], x2t_d, reads=[BX2T], writes=[BX2])
                    HB = [P.sbuf(f"HB{i}", [128, 1060], F32, ph9a) for i in range(2)]
                    CB2 = [P.sbuf(f"CB2{i}", [128, NQ], F32, ph9a) for i in range(2)]
                    wi = [0]

                    def wload(col0):
                        t, Bt = WS[wi[0] % 3]
                        if wi[0] >= 3:
                            P.dma("pool", t[:], kp(w_up_d[:, col0:col0 + 512]), writes=[Bt])
                        wi[0] += 1
                        return t, Bt
                    for t in range(11):
                        tgw = wload(t * 512)
                        tvw = wload(DFF + t * 512)
                        for m4 in range(4):
                            i = t * 4 + m4
                            for gi, (wt, Bwt) in enumerate([tgw, tvw]):
                                ci = gi * 44 + i
                                hb, Bhb = HB[gi]
                                cb, Bcb = CB2[gi]
                                P.op("act", lambda e: e.copy(out=hb[:, 1026:1028], in_=CF[:, ci, :]), reads=[BCF],
                                     writes=[Bhb])
                                for (c0, n) in [(0, 512), (512, 512), (1024, 34)]:
                                    ps, Bps = nextps()
                                    mm(P, ps[:, :n], [(wt[:, kc, m4 * 128:(m4 + 1) * 128], X2[:, kc, c0:c0 + n])
                                                      for kc in range(16)], [Bwt, BX2], [Bps])
                                    if c0 < 1024:
                                        evac(hb[:, 2 + c0:2 + c0 + n], ps[:, :n], [Bps], [Bhb])
                                    else:
                                        P.op("dve", lambda e: e.tensor_scalar(out=hb[:, 0:2], in0=ps[:, 0:2],
                                                                              scalar1=hmask[:, 0:1], scalar2=None,
                                                                              op0=ALU.mult),
                                             reads=[Bps, Bhmask], writes=[Bhb])
                                        evac(hb[:, 1028:1060], ps[:, 2:34], [Bps], [Bhb])
                                P.op("act", lambda e: e.copy(out=HOUT[:, ci, 0:2], in_=hb[:, 1024:1026]), reads=[Bhb],
                                     writes=[BHOUT])
                                P.op("act", lambda e: e.copy(out=HOUT[:, ci, 2:4], in_=hb[:, 1058:1060]), reads=[Bhb],
                                     writes=[BHOUT])
                                P.op("act", lambda e: e.activation(out=cb[:, :], in_=hb[:, 2:1060], func=AF.Identity,
                                                                   scale=wff[:, ci, 2:3], bias=bff[:, ci:ci + 1]),
                                     reads=[Bhb, Bwff, Bbff], writes=[Bcb])
                                P.op("dve", lambda e: e.scalar_tensor_tensor(out=cb[:, :], in0=hb[:, 1:1059],
                                                                             scalar=wff[:, ci, 1:2], in1=cb[:, :],
                                                                             op0=ALU.mult, op1=ALU.add),
                                     reads=[Bhb, Bwff, Bcb], writes=[Bcb])
                                P.op("dve", lambda e: e.scalar_tensor_tensor(out=cb[:, :], in0=hb[:, 0:1058],
                                                                             scalar=wff[:, ci, 0:1], in1=cb[:, :],
                                                                             op0=ALU.mult, op1=ALU.add),
                                     reads=[Bhb, Bwff, Bcb], writes=[Bcb])
                            cg_, Bcg_ = CB2[0]
                            cv_, Bcv_ = CB2[1]
                            P.op("act", lambda e: e.activation(out=cg_[:, :], in_=cg_[:, :], func=AF.Silu),
                                 reads=[Bcg_], writes=[Bcg_])
                            P.op("dve", lambda e: e.tensor_tensor(out=ACTT[:, i, :], in0=cg_[:, :], in1=cv_[:, :],
                                                                  op=ALU.mult), reads=[Bcg_, Bcv_], writes=[BACTT])
                    P.dma("sp", ffnT_o.rearrange("(c p) t -> p c t", p=128), HOUT[:], reads=[BHOUT],
                          writes=[OUTB["ffnTo"]], sembuf=BHOUT)
                    KT = [(0, 16), (16, 16), (32, 12)]
                    WD0 = [WS[1], WS[2], WS[0]]
                    for kt, (k0, nk) in enumerate(KT):
                        P.dma("pool", WD0[kt][0][:, 0:nk, :],
                              w_down_d[k0 * 128:(k0 + nk) * 128, 0:512].rearrange("(kc p) n -> p kc n", p=128),
                              writes=[WD0[kt][1]])
                    P.barrier()
                P.barrier()
                CK(95)
                with ExitStack() as ph9b:
                    XB3 = [P.sbuf(f"XB3{i}", [128, 512], F32, ph9b) for i in range(3)]
                    Z3 = [P.sbuf(f"Z3{i}", [128, 512], F32, ph9b) for i in range(3)]
                    WD = WD0 + [P.sbuf(f"WD{i}", [128, 16, 512], BF16, ph9b) for i in range(3)]

                    def wd_load(g):
                        for kt, (k0, nk) in enumerate(KT):
                            t_, Bt_ = WD[(g % 2) * 3 + kt]
                            P.dma("pool", t_[:, 0:nk, :],
                                  w_down_d[k0 * 128:(k0 + nk) * 128, g * 512:(g + 1) * 512].rearrange(
                                      "(kc p) n -> p kc n", p=128), writes=[Bt_])
                    steps = [(g, ti) for g in range(4) for ti in range(len(TBLK))]

                    def ld(si):
                        g_, ti_ = steps[si]
                        c0_, nt_ = TBLK[ti_]
                        xb_, Bxb_ = XB3[si % 3]
                        P.dma("sp", xb_[0:nt_, :], xres_d[c0_:c0_ + nt_, g_ * 512:(g_ + 1) * 512],
                              reads=[SCR["xres"]], writes=[Bxb_])
                    ld(0)
                    for g in range(4):
                        if g + 1 < 4:
                            wd_load(g + 1)
                        wset = WD[(g % 2) * 3:(g % 2) * 3 + 3]
                        for ti, (c0, nt) in enumerate(TBLK):
                            si = g * len(TBLK) + ti
                            if si + 1 < len(steps):
                                ld(si + 1)
                            xb, Bxb = XB3[si % 3]
                            z3, Bz3 = Z3[si % 3]
                            ps, Bps = nextps()
                            for kt, (k0, nk) in enumerate(KT):
                                def fk(e, kt=kt, k0=k0, nk=nk):
                                    for kc in range(nk):
                                        inst = e.matmul(ps[0:nt, :512], lhsT=ACTT[:, k0 + kc, c0:c0 + nt],
                                                        rhs=wset[kt][0][:, kc, :], start=(kt == 0 and kc == 0),
                                                        stop=(kt == 2 and kc == nk - 1))
                                    return inst
                                P.op("pe", fk, [BACTT, wset[kt][1]], [Bps])
                            P.op("dve", lambda e: e.scalar_tensor_tensor(out=z3[0:nt, :], in0=xb[0:nt, :], scalar=ALPHA,
                                                                         in1=ps[0:nt, :512], op0=ALU.mult, op1=ALU.add),
                                 reads=[Bxb, Bps], writes=[Bz3])
                            P.dma("sp", zscr_d[c0:c0 + nt, g * 512:(g + 1) * 512], z3[0:nt, :], reads=[Bz3],
                                  writes=[BZSCR], sembuf=Bz3)
                    P.barrier()
                P.barrier()
            P.barrier()
            CK(97)
            with ExitStack() as ph10:
                proj_ln(None, None, None, 4, None, None, y_d, OUTB["y"], ph10, "c", zsrc_d=zscr_d)
                P.barrier()

        except StopBuild:
            pass
        P.finish(list(OUTB.values()) + list(SCR.values()))
    return nc


_NC_CACHE = {}


def _prep_inputs(inp):
    f = lambda a: np.ascontiguousarray(np.asarray(a, dtype=np.float32))
    xp = f(inp["x_prompt"])[0]
    xsm = f(inp["x_sample"])
    z2 = np.zeros((2, D), np.float32)
    xr = np.concatenate([z2, xp[:NREM - 2]], 0)
    xrT = np.ascontiguousarray(xr.T)
    shared = {
        "xrT": xrT,
        "w_in": f(inp["w_in"])[0], "w_out": f(inp["w_out"])[0], "w_mq": f(inp["w_mq"])[0],
        "w_mk": f(inp["w_mk"])[0], "w_mv": f(inp["w_mv"])[0], "w_mo": f(inp["w_mo"])[0],
        "w_up": f(inp["w_up"])[0], "w_down": f(inp["w_down"])[0],
        "wconvT": np.ascontiguousarray(f(inp["w_conv"])[0].T.reshape(8, 128, 31).transpose(1, 0, 2)),
        "cvec": np.ascontiguousarray(np.stack([f(inp["b_conv"])[0], f(inp["g_conv_ln"])[0], f(inp["b_conv_ln"])[0]],
                                              -1).reshape(8, 128, 3).transpose(1, 0, 2)),
        "wffnT": np.ascontiguousarray(f(inp["w_ffn_conv"])[0].T.reshape(88, 128, 3).transpose(1, 0, 2)),
        "bffn": np.ascontiguousarray(f(inp["b_ffn_conv"])[0].reshape(88, 128).T),
        "lnp": np.ascontiguousarray(np.broadcast_to(
            np.stack([f(inp[k])[0] for k in ["g_ln1", "b_ln1", "g_ln2", "b_ln2", "g_ln3", "b_ln3"]])[:, None, :],
            (6, 128, D))),
        "bfb": np.ascontiguousarray(np.broadcast_to(f(inp["b_forget"])[0][None, :], (128, 16))),
        "ident": np.eye(128, dtype=np.float32),
        "tri": np.triu(np.ones((128, 128), np.float32)),
        "dneg": np.concatenate([np.full((2, 1), NEG, np.float32), np.zeros((126, 1), np.float32)], 0),
        "memT": np.ascontiguousarray(f(inp["mem_prompt"])[0].T),
        "shift": np.eye(128, dtype=np.float32)[64:128].T[0:64].copy() if False else np.concatenate([np.zeros((64, 64), np.float32), np.eye(64, dtype=np.float32)], 1),
    }
    maps = []
    for c in range(NCORE):
        own = xp[OWN * c:OWN * (c + 1)]
        if c > 0:
            halo = xp[OWN * c - 2:OWN * c]
            ch = xp[OWN * c - 32:OWN * c - 2]
        else:
            halo = np.zeros((2, D), np.float32)
            ch = np.zeros((30, D), np.float32)
        xloc = np.concatenate([own, halo, xsm[c], ch], 0)
        m = dict(shared)
        m["xtm"] = np.ascontiguousarray(xloc)
        m["xT"] = np.ascontiguousarray(xloc.T)
        m["valid"] = np.ascontiguousarray(np.broadcast_to((np.arange(8) < c).astype(np.float32)[None, :], (128, 8)))
        m["hmask"] = np.full((128, 1), 0.0 if c == 0 else 1.0, np.float32)
        m["ckT"] = np.ascontiguousarray(f(inp["cache_fox_k"])[0, c].reshape(2048, 1024).T)
        m["cv"] = np.ascontiguousarray(f(inp["cache_fox_v"])[0, c].reshape(2048, 1024))
        m["clf"] = np.ascontiguousarray(f(inp["cache_fox_logf"])[0, c])
        m["ccT"] = np.ascontiguousarray(f(inp["cache_conv"])[0, c].T)
        m["cffT"] = np.ascontiguousarray(f(inp["cache_ffn"])[0, c].T)
        m["cmkT"] = np.ascontiguousarray(f(inp["cache_mem_k"])[0, c].reshape(256, D).T)
        m["cmv"] = np.ascontiguousarray(f(inp["cache_mem_v"])[0, c].reshape(256, D))
        maps.append(m)
    return maps


def _assemble(res):
    R = res
    yp = np.concatenate([R[c]["y"][:OWN] for c in range(NCORE)], 0)[None]
    ys = np.stack([R[c]["y"][SMP0:SMP0 + 32] for c in range(NCORE)], 0)
    p_conv = np.ascontiguousarray(R[7]["convTo"][:, 0:30].T)[None, None]
    pk = np.concatenate([R[c]["kTo"][:, :OWN].T for c in range(NCORE)], 0).reshape(1, 1, 8192, 16, 64)
    pv = np.concatenate([R[c]["vo"][:OWN] for c in range(NCORE)], 0).reshape(1, 1, 8192, 16, 64)
    plf = np.concatenate([R[c]["lfo"][:OWN] for c in range(NCORE)], 0).reshape(1, 1, 8192, 16)
    pmk = R[0]["memk"].reshape(1, 1, 256, 4, 512)
    pmv = R[0]["memv"].reshape(1, 1, 256, 4, 512)
    pf = np.ascontiguousarray(R[7]["ffnTo"][:, 0:2].T)[None, None]
    s_conv = np.stack([R[c]["convTo"][:, 30:60].T for c in range(NCORE)], 0)[None]
    sk = np.stack([R[c]["kTo"][:, SMP0:SMP0 + 32].T for c in range(NCORE)], 0).reshape(1, 8, 32, 16, 64)
    sv = np.stack([R[c]["vo"][SMP0:SMP0 + 32] for c in range(NCORE)], 0).reshape(1, 8, 32, 16, 64)
    slf = np.stack([R[c]["lfo"][SMP0:SMP0 + 32] for c in range(NCORE)], 0).reshape(1, 8, 32, 16)
    sf = np.stack([R[c]["ffnTo"][:, 2:4].T for c in range(NCORE)], 0)[None]
    outs = (yp, ys, p_conv, pk, pv, plf, pmk, pmv, pf, s_conv, sk, sv, slf, sf)
    return tuple(np.ascontiguousarray(o, dtype=np.float32) for o in outs)


def kernel(**inputs):
    if "nc" not in _NC_CACHE:
        _NC_CACHE["nc"] = build_nc()
    nc = _NC_CACHE["nc"]
    maps = _prep_inputs(inputs)
    res = run_bass_kernel_spmd(nc, maps, core_ids=list(range(NCORE)))
    return _assemble(res.results)
```
